# Optimizing a Trainium2 kernel written in Bass

```python
import math
import jax, jax.numpy as jnp
from jax import lax
import numpy as np

D_MODEL = 1024
BATCH = 8
SEQ = 2048
DEPTH = 2

CHUNK = 64
Q_BLOCK = 128
RMS_EPS = 1e-6
LN_EPS = 1e-5
ROPE_THETA = 500000.0
ROPE_FRACTION = 4
DIFF_HEADS = D_MODEL // 256
DIFF_HEAD_DIM = 64
DIFF_WIDTH = DIFF_HEADS * 2 * DIFF_HEAD_DIM
CONV_CHANNELS = D_MODEL - DIFF_WIDTH
CONV_WIDTH = 31
SSM_CHANNELS = D_MODEL // 4
SSM_GROUP = 16
SSM_GROUPS = SSM_CHANNELS // SSM_GROUP
SSM_STATE = 64
DT_MIN = 1e-3
DT_MAX = 1e-1
MLA_V_DIM = 128
MLA_HEADS = (D_MODEL - SSM_CHANNELS) // MLA_V_DIM
MLA_NOPE_DIM = 128
MLA_ROPE_DIM = 64
MLA_Q_RANK = D_MODEL // 4
MLA_KV_RANK = D_MODEL // 8
MLA_ROPE_THETA = 10000.0
D_FF = -(-8 * D_MODEL // (3 * 256)) * 256
EVEN_IN = 3 * DIFF_WIDTH + 2 * CONV_CHANNELS
ODD_IN = SSM_CHANNELS + MLA_Q_RANK + MLA_KV_RANK + MLA_ROPE_DIM

kernel_name = 'chunk_causal_hybrid_diffattn_conformer_s5_mla'


def rmsnorm(x, g):
    x32 = x.astype(jnp.float32)
    y = x32 * lax.rsqrt(jnp.mean(x32 * x32, axis=-1, keepdims=True) + RMS_EPS)
    return (y * g.astype(jnp.float32)).astype(x.dtype)


def layernorm(x, g, b):
    x32 = x.astype(jnp.float32)
    mu = jnp.mean(x32, axis=-1, keepdims=True)
    xc = x32 - mu
    y = xc * lax.rsqrt(jnp.mean(xc * xc, axis=-1, keepdims=True) + LN_EPS)
    return (y * g.astype(jnp.float32) + b.astype(jnp.float32)).astype(x.dtype)


def rope_tables(s, rot_dim, theta):
    inv = theta ** (-jnp.arange(0, rot_dim, 2, dtype=jnp.float32) / rot_dim)
    ang = jnp.arange(s, dtype=jnp.float32)[:, None] * inv[None, :]
    return jnp.cos(ang), jnp.sin(ang)


def apply_rope(x, cos, sin):
    half = cos.shape[-1]
    shape = (1, cos.shape[0]) + (1,) * (x.ndim - 3) + (half,)
    c = cos.reshape(shape).astype(x.dtype)
    s = sin.reshape(shape).astype(x.dtype)
    x1 = x[..., :half]
    x2 = x[..., half:2 * half]
    return jnp.concatenate([x1 * c - x2 * s, x2 * c + x1 * s, x[..., 2 * half:]], axis=-1)


def chunk_causal_mask(i, s):
    qpos = i * Q_BLOCK + jnp.arange(Q_BLOCK)
    kpos = jnp.arange(s)
    return (kpos // CHUNK)[None, :] <= (qpos // CHUNK)[:, None]


def sweep_query_blocks(block_fn, q):
    b, s = q.shape[:2]
    nb = s // Q_BLOCK
    qb = jnp.moveaxis(q.reshape((b, nb, Q_BLOCK) + q.shape[2:]), 1, 0)
    out = lax.map(lambda a: block_fn(a[0], a[1]), (jnp.arange(nb), qb))
    out = jnp.moveaxis(out, 0, 1)
    return out.reshape((b, s) + out.shape[3:])


def sandwich(x, g_pre, g_post, fn):
    return x + rmsnorm(fn(rmsnorm(x, g_pre)), g_post)


def swiglu(h, w_gate, w_up, w_down):
    return (jax.nn.silu(h @ w_gate) * (h @ w_up)) @ w_down


def diff_attention(qkv, layer_idx, lq1, lk1, lq2, lk2, subln, cos, sin):
    b, s, _ = qkv.shape
    f32 = jnp.float32
    q = qkv[..., :DIFF_WIDTH].reshape(b, s, DIFF_HEADS, 2, DIFF_HEAD_DIM)
    k = qkv[..., DIFF_WIDTH:2 * DIFF_WIDTH].reshape(b, s, DIFF_HEADS, 2, DIFF_HEAD_DIM)
    v = qkv[..., 2 * DIFF_WIDTH:].reshape(b, s, DIFF_HEADS, 2 * DIFF_HEAD_DIM)
    q = apply_rope(q, cos, sin)
    k = apply_rope(k, cos, sin)
    lam_init = 0.8 - 0.6 * math.exp(-0.3 * layer_idx)
    lam = (jnp.exp(jnp.sum(lq1.astype(f32) * lk1.astype(f32)))
           - jnp.exp(jnp.sum(lq2.astype(f32) * lk2.astype(f32))) + lam_init)
    scale = DIFF_HEAD_DIM ** -0.5

    def block(i, qb):
        sc = jnp.einsum('bqhcd,bkhcd->bhcqk', qb, k, preferred_element_type=f32) * scale
        p = jax.nn.softmax(jnp.where(chunk_causal_mask(i, s), sc, -jnp.inf), axis=-1)
        w = p[:, :, 0] - lam * p[:, :, 1]
        return jnp.einsum('bhqk,bkhe->bqhe', w.astype(v.dtype), v)

    o = sweep_query_blocks(block, q)
    o = rmsnorm(o, subln) * (1.0 - lam_init)
    return o.reshape(b, s, DIFF_WIDTH)


def conformer_conv(g, dw_w, dw_b, ln_g, ln_b):
    u = g[..., :CONV_CHANNELS] * jax.nn.sigmoid(g[..., CONV_CHANNELS:])
    y = lax.conv_general_dilated(
        u, dw_w[:, None, :].astype(u.dtype), window_strides=(1,),
        padding=[(CONV_WIDTH - 1, 0)], dimension_numbers=('NWC', 'WIO', 'NWC'),
        feature_group_count=CONV_CHANNELS) + dw_b
    return jax.nn.silu(layernorm(y, ln_g, ln_b))


def s5_ssm(u, a_re, a_im, log_dt, b_re, b_im, c_re, c_im, d_skip, w_glu, b_glu):
    b, s, _ = u.shape
    f32 = jnp.float32
    ug = u.reshape(b, s, SSM_GROUPS, SSM_GROUP).astype(f32)
    lr = a_re.astype(f32)
    li = a_im.astype(f32)
    dt = jnp.exp(log_dt.astype(f32))[:, None]
    mag = jnp.exp(lr * dt)
    ab_re = mag * jnp.cos(li * dt)
    ab_im = mag * jnp.sin(li * dt)
    den = lr * lr + li * li
    n_re = ab_re - 1.0
    f_re = (n_re * lr + ab_im * li) / den
    f_im = (ab_im * lr - n_re * li) / den
    br = b_re.astype(f32)
    bi = b_im.astype(f32)
    bb_re = f_re[:, :, None] * br - f_im[:, :, None] * bi
    bb_im = f_re[:, :, None] * bi + f_im[:, :, None] * br
    bu_re = jnp.einsum('bsgh,gph->bsgp', ug, bb_re)
    bu_im = jnp.einsum('bsgh,gph->bsgp', ug, bb_im)
    aa_re = jnp.broadcast_to(ab_re, bu_re.shape)
    aa_im = jnp.broadcast_to(ab_im, bu_im.shape)

    def combine(e1, e2):
        ar1, ai1, xr1, xi1 = e1
        ar2, ai2, xr2, xi2 = e2
        return (ar2 * ar1 - ai2 * ai1, ar2 * ai1 + ai2 * ar1,
                ar2 * xr1 - ai2 * xi1 + xr2, ar2 * xi1 + ai2 * xr1 + xi2)

    _, _, x_re, x_im = lax.associative_scan(combine, (aa_re, aa_im, bu_re, bu_im), axis=1)
    y = (jnp.einsum('bsgp,ghp->bsgh', x_re, c_re.astype(f32))
         - jnp.einsum('bsgp,ghp->bsgh', x_im, c_im.astype(f32))
         + d_skip.astype(f32)[None, None] * ug)
    y = y.reshape(b, s, SSM_CHANNELS).astype(u.dtype)
    z = jax.nn.gelu(y)
    return z * jax.nn.sigmoid(z @ w_glu + b_glu)


def mla_attention(c_q, c_kv, k_r, q_norm, w_uq, kv_norm, w_ukv, cos, sin):
    b, s, _ = c_q.shape
    f32 = jnp.float32
    q = (rmsnorm(c_q, q_norm) @ w_uq).reshape(b, s, MLA_HEADS, MLA_NOPE_DIM + MLA_ROPE_DIM)
    q = jnp.concatenate([q[..., :MLA_NOPE_DIM], apply_rope(q[..., MLA_NOPE_DIM:], cos, sin)], axis=-1)
    kv = (rmsnorm(c_kv, kv_norm) @ w_ukv).reshape(b, s, MLA_HEADS, MLA_NOPE_DIM + MLA_V_DIM)
    k_rope = jnp.broadcast_to(apply_rope(k_r, cos, sin)[:, :, None, :], (b, s, MLA_HEADS, MLA_ROPE_DIM))
    k = jnp.concatenate([kv[..., :MLA_NOPE_DIM], k_rope], axis=-1)
    v = kv[..., MLA_NOPE_DIM:]
    scale = (MLA_NOPE_DIM + MLA_ROPE_DIM) ** -0.5

    def block(i, qb):
        sc = jnp.einsum('bqhd,bkhd->bhqk', qb, k, preferred_element_type=f32) * scale
        p = jax.nn.softmax(jnp.where(chunk_causal_mask(i, s), sc, -jnp.inf), axis=-1)
        return jnp.einsum('bhqk,bkhd->bqhd', p.astype(v.dtype), v)

    o = sweep_query_blocks(block, q)
    return o.reshape(b, s, MLA_HEADS * MLA_V_DIM)


def even_mixer(t, layer_idx, w_in, lq1, lk1, lq2, lk2, subln, dw_w, dw_b, ln_g, ln_b, w_out, cos, sin):
    proj = t @ w_in
    y_a = diff_attention(proj[..., :3 * DIFF_WIDTH], layer_idx, lq1, lk1, lq2, lk2, subln, cos, sin)
    y_b = conformer_conv(proj[..., 3 * DIFF_WIDTH:], dw_w, dw_b, ln_g, ln_b)
    return jnp.concatenate([y_a, y_b], axis=-1) @ w_out


def odd_mixer(t, w_in, a_re, a_im, log_dt, b_re, b_im, c_re, c_im, d_skip, w_glu, b_glu,
              q_norm, w_uq, kv_norm, w_ukv, w_out, cos, sin):
    proj = t @ w_in
    o1 = SSM_CHANNELS
    o2 = o1 + MLA_Q_RANK
    o3 = o2 + MLA_KV_RANK
    y_c = s5_ssm(proj[..., :o1], a_re, a_im, log_dt, b_re, b_im, c_re, c_im, d_skip, w_glu, b_glu)
    y_d = mla_attention(proj[..., o1:o2], proj[..., o2:o3], proj[..., o3:], q_norm, w_uq, kv_norm, w_ukv, cos, sin)
    return jnp.concatenate([y_c, y_d], axis=-1) @ w_out


def setup_inputs(seed: int = 0) -> dict:
    key = jax.random.key(seed)
    ks = iter(jax.random.split(key, 64))
    f32 = jnp.float32

    def nrm(shape, scale):
        return jax.random.normal(next(ks), shape, f32) * scale

    def gain(n):
        return 1.0 + 0.02 * jax.random.normal(next(ks), (n,), f32)

    d = D_MODEL
    inp = {}
    inp['x'] = nrm((BATCH, SEQ, d), 1.0)
    inp['l0_mix_pre'] = gain(d)
    inp['l0_mix_post'] = gain(d)
    inp['l0_w_in'] = nrm((d, EVEN_IN), d ** -0.5)
    inp['l0_lambda_q1'] = nrm((DIFF_HEAD_DIM,), 0.1)
    inp['l0_lambda_k1'] = nrm((DIFF_HEAD_DIM,), 0.1)
    inp['l0_lambda_q2'] = nrm((DIFF_HEAD_DIM,), 0.1)
    inp['l0_lambda_k2'] = nrm((DIFF_HEAD_DIM,), 0.1)
    inp['l0_subln'] = gain(2 * DIFF_HEAD_DIM)
    inp['l0_dw_w'] = nrm((CONV_WIDTH, CONV_CHANNELS), CONV_WIDTH ** -0.5)
    inp['l0_dw_b'] = nrm((CONV_CHANNELS,), 0.02)
    inp['l0_conv_ln_g'] = gain(CONV_CHANNELS)
    inp['l0_conv_ln_b'] = nrm((CONV_CHANNELS,), 0.02)
    inp['l0_w_out'] = nrm((DIFF_WIDTH + CONV_CHANNELS, d), (DIFF_WIDTH + CONV_CHANNELS) ** -0.5)
    inp['l0_ffn_pre'] = gain(d)
    inp['l0_ffn_post'] = gain(d)
    inp['l0_w_gate'] = nrm((d, D_FF), d ** -0.5)
    inp['l0_w_up'] = nrm((d, D_FF), d ** -0.5)
    inp['l0_w_down'] = nrm((D_FF, d), D_FF ** -0.5)
    inp['l1_mix_pre'] = gain(d)
    inp['l1_mix_post'] = gain(d)
    inp['l1_w_in'] = nrm((d, ODD_IN), d ** -0.5)
    inp['l1_a_re'] = -0.5 + nrm((SSM_GROUPS, SSM_STATE), 0.01)
    inp['l1_a_im'] = (jnp.pi * jnp.arange(SSM_STATE, dtype=f32))[None, :] + nrm((SSM_GROUPS, SSM_STATE), 0.01)
    inp['l1_log_dt'] = jax.random.uniform(next(ks), (SSM_GROUPS,), f32, math.log(DT_MIN), math.log(DT_MAX))
    inp['l1_b_re'] = nrm((SSM_GROUPS, SSM_STATE, SSM_GROUP), (2 * SSM_GROUP) ** -0.5)
    inp['l1_b_im'] = nrm((SSM_GROUPS, SSM_STATE, SSM_GROUP), (2 * SSM_GROUP) ** -0.5)
    inp['l1_c_re'] = nrm((SSM_GROUPS, SSM_GROUP, SSM_STATE), (2 * SSM_STATE) ** -0.5)
    inp['l1_c_im'] = nrm((SSM_GROUPS, SSM_GROUP, SSM_STATE), (2 * SSM_STATE) ** -0.5)
    inp['l1_d_skip'] = nrm((SSM_GROUPS, SSM_GROUP), 1.0)
    inp['l1_w_glu'] = nrm((SSM_CHANNELS, SSM_CHANNELS), SSM_CHANNELS ** -0.5)
    inp['l1_b_glu'] = nrm((SSM_CHANNELS,), 0.02)
    inp['l1_q_norm'] = gain(MLA_Q_RANK)
    inp['l1_w_uq'] = nrm((MLA_Q_RANK, MLA_HEADS * (MLA_NOPE_DIM + MLA_ROPE_DIM)), MLA_Q_RANK ** -0.5)
    inp['l1_kv_norm'] = gain(MLA_KV_RANK)
    inp['l1_w_ukv'] = nrm((MLA_KV_RANK, MLA_HEADS * (MLA_NOPE_DIM + MLA_V_DIM)), MLA_KV_RANK ** -0.5)
    inp['l1_w_out'] = nrm((SSM_CHANNELS + MLA_HEADS * MLA_V_DIM, d), (SSM_CHANNELS + MLA_HEADS * MLA_V_DIM) ** -0.5)
    inp['l1_ffn_pre'] = gain(d)
    inp['l1_ffn_post'] = gain(d)
    inp['l1_w_gate'] = nrm((d, D_FF), d ** -0.5)
    inp['l1_w_up'] = nrm((d, D_FF), d ** -0.5)
    inp['l1_w_down'] = nrm((D_FF, d), D_FF ** -0.5)
    return inp


def reference(x,
              l0_mix_pre, l0_mix_post, l0_w_in, l0_lambda_q1, l0_lambda_k1, l0_lambda_q2, l0_lambda_k2,
              l0_subln, l0_dw_w, l0_dw_b, l0_conv_ln_g, l0_conv_ln_b, l0_w_out,
              l0_ffn_pre, l0_ffn_post, l0_w_gate, l0_w_up, l0_w_down,
              l1_mix_pre, l1_mix_post, l1_w_in, l1_a_re, l1_a_im, l1_log_dt, l1_b_re, l1_b_im,
              l1_c_re, l1_c_im, l1_d_skip, l1_w_glu, l1_b_glu, l1_q_norm, l1_w_uq, l1_kv_norm, l1_w_ukv,
              l1_w_out, l1_ffn_pre, l1_ffn_post, l1_w_gate, l1_w_up, l1_w_down):
    s = x.shape[1]
    cos_a, sin_a = rope_tables(s, DIFF_HEAD_DIM // ROPE_FRACTION, ROPE_THETA)
    cos_d, sin_d = rope_tables(s, MLA_ROPE_DIM, MLA_ROPE_THETA)

    def layer0(h):
        h = sandwich(h, l0_mix_pre, l0_mix_post, lambda t: even_mixer(
            t, 0, l0_w_in, l0_lambda_q1, l0_lambda_k1, l0_lambda_q2, l0_lambda_k2, l0_subln,
            l0_dw_w, l0_dw_b, l0_conv_ln_g, l0_conv_ln_b, l0_w_out, cos_a, sin_a))
        return sandwich(h, l0_ffn_pre, l0_ffn_post, lambda t: swiglu(t, l0_w_gate, l0_w_up, l0_w_down))

    def layer1(h):
        h = sandwich(h, l1_mix_pre, l1_mix_post, lambda t: odd_mixer(
            t, l1_w_in, l1_a_re, l1_a_im, l1_log_dt, l1_b_re, l1_b_im, l1_c_re, l1_c_im, l1_d_skip,
            l1_w_glu, l1_b_glu, l1_q_norm, l1_w_uq, l1_kv_norm, l1_w_ukv, l1_w_out, cos_d, sin_d))
        return sandwich(h, l1_ffn_pre, l1_ffn_post, lambda t: swiglu(t, l1_w_gate, l1_w_up, l1_w_down))

    layers = (layer0, layer1)
    h = x
    for i in range(DEPTH):
        h = layers[i](h)
    return h
```

```python
import numpy as np
import concourse.bass as bass
import concourse.mybir as mybir

SEM_CAP = 30000
STRICT = True


def _box(ap):
    t = ap.tensor
    name = t.name
    dims = [list(d) for d in ap.ap]
    off = int(ap.offset)
    space = str(ap.space)
    if 'DRAM' in space.upper() or 'HBM' in space.upper():
        span = sum((c - 1) * abs(s) for s, c in dims) + 1
        return (name, 0, 1, off, off + span)
    shp = list(t.shape)
    per = 1
    for s in shp[1:]:
        per *= s
    pstep, pcnt = dims[0]
    p0 = off // per
    f0 = off % per
    if pstep == 0:
        pcnt_eff = 1
    else:
        pcnt_eff = pcnt
    span = sum((c - 1) * abs(s) for s, c in dims[1:]) + 1
    f1 = f0 + span
    if 'PSUM' in space.upper():
        eb = 2048 // mybir.dt.size(ap.dtype) if hasattr(mybir.dt, 'size') else 512
        f0 = (f0 // eb) * eb
        f1 = ((f1 + eb - 1) // eb) * eb
        return (name, 0, 128, f0, f1)
    return (name, p0, p0 + pcnt_eff, f0, f1)


def _ov(a, b):
    return a[1] < b[2] and b[1] < a[2] and a[3] < b[4] and b[3] < a[4]


def _cov(a, b):
    return a[1] <= b[1] and a[2] >= b[2] and a[3] <= b[3] and a[4] >= b[4]


class Prog:
    ENG = ('pe', 'act', 'dve', 'pool', 'sp')

    def __init__(self, nc):
        self.nc = nc
        self.eng = {'pe': nc.tensor, 'act': nc.scalar, 'dve': nc.vector,
                    'pool': nc.gpsimd, 'sp': nc.sync}
        self.ins = []

    def op(self, e, fn, r, w):
        self.ins.append(dict(e=e, fn=fn, r=[_box(a) for a in r], w=[_box(a) for a in w], dma=False))

    def barrier(self):
        self.ins.append(dict(e=None, fn=None, r=[], w=[], dma=False, bar=True))

    def dma(self, out, in_, q='sp'):
        def fn(eng, out=out, in_=in_):
            return eng.dma_start(out=out, in_=in_)
        self.ins.append(dict(e=q, fn=fn, r=[_box(in_)], w=[_box(out)], dma=True))

    def finalize(self, sems_pool):
        nc = self.nc
        ins = self.ins
        n = len(ins)
        hist = {}
        seq = {}
        cnt = {e: 0 for e in self.ENG}
        observed = {e: {p: -1 for p in self.ENG} for e in self.ENG}
        obs_dma = {e: set() for e in self.ENG}
        snap = {}
        waits = [[] for _ in range(n)]
        signal = [False] * n
        eng_list = {e: [] for e in self.ENG}
        for i, I in enumerate(ins):
            if I.get('bar'):
                bw = []
                for E in self.ENG:
                    for Pn in self.ENG:
                        if cnt[Pn] == 0 or (Pn == E and E in ('sp', 'pe')):
                            continue
                        sp_ = cnt[Pn] - 1
                        if observed[E][Pn] >= sp_:
                            continue
                        bw.append((E, Pn, sp_))
                        signal[eng_list[Pn][sp_]] = True
                        observed[E][Pn] = sp_
                    obs_dma[E] = set(j for j in range(i) if ins[j]['dma'])
                waits[i] = bw
                hist = {}
                continue
            E = I['e']
            s = cnt[E]
            cnt[E] += 1
            seq[i] = (E, s)
            eng_list[E].append(i)
            deps = set()
            for b in I['r']:
                for (hb, j, isw) in hist.get(b[0], ()):
                    if isw and _ov(hb, b):
                        deps.add(j)
            for b in I['w']:
                for (hb, j, isw) in hist.get(b[0], ()):
                    if _ov(hb, b):
                        deps.add(j)
            rawset = set()
            for b in I['r']:
                for (hb, j, isw) in hist.get(b[0], ()):
                    if isw and _ov(hb, b):
                        rawset.add(j)
            need = {}
            for j in deps:
                J = ins[j]
                if J['dma']:
                    if j not in obs_dma[E]:
                        waits[i].append(('d', j))
                        obs_dma[E].add(j)
                        signal[j] = True
                        sj = snap[j]
                        for p in self.ENG:
                            if sj[p] > observed[E][p]:
                                observed[E][p] = sj[p]
                    continue
                P, sp_ = seq[j]
                if P == E and not I['dma']:
                    if E == 'pe':
                        continue
                    if (not STRICT) and j not in rawset:
                        continue
                if sp_ > need.get(P, -1):
                    need[P] = sp_
            for P, sp_ in need.items():
                if observed[E][P] >= sp_:
                    continue
                j = eng_list[P][sp_]
                waits[i].append(('e', P, sp_))
                signal[j] = True
                observed[E][P] = sp_
                sj = snap[j]
                for p in self.ENG:
                    if sj[p] > observed[E][p]:
                        observed[E][p] = sj[p]
            snap[i] = dict(observed[E])
            for b in I['w']:
                lst = hist.setdefault(b[0], [])
                lst[:] = [h for h in lst if not _cov(b, h[0])]
                lst.append((b, i, True))
            for b in I['r']:
                lst = hist.setdefault(b[0], [])
                if not I['dma']:
                    lst[:] = [h for h in lst if h[2] or ins[h[1]]['dma'] or ins[h[1]]['e'] != E or not _cov(b, h[0])]
                lst.append((b, i, False))
        sem_iter = iter(sems_pool)
        eng_sem = {}
        eng_sig = {e: 0 for e in self.ENG}
        sem_of = {}
        for e in self.ENG:
            eng_sem[e] = next(sem_iter)
        dma_sems = [next(sem_iter) for _ in range(64)]
        dma_cnt = [0] * len(dma_sems)
        dma_last = [None] * len(dma_sems)
        dma_rr = {'sp': 0, 'pool': 0}
        bar_seen = {e: [0] * len(dma_sems) for e in self.ENG}
        dma_rng = {'sp': (0, 32), 'pool': (32, 64)}
        final_dma = []
        nw = 0
        for i, I in enumerate(ins):
            if I.get('bar'):
                for (E, Pn, sp_) in waits[i]:
                    sm, val = sem_of[eng_list[Pn][sp_]]
                    self.eng[E].wait_ge(sm, val)
                    nw += 1
                for E in self.ENG:
                    for k, sm in enumerate(dma_sems):
                        if dma_cnt[k] > bar_seen[E][k]:
                            self.eng[E].wait_ge(sm, dma_cnt[k])
                            bar_seen[E][k] = dma_cnt[k]
                continue
            E = I['e']
            eng = self.eng[E]
            for wt in waits[i]:
                if wt[0] == 'd':
                    sm, val = sem_of[wt[1]]
                else:
                    sm, val = sem_of[eng_list[wt[1]][wt[2]]]
                eng.wait_ge(sm, val)
                nw += 1
            if I['dma']:
                lo_, hi_ = dma_rng[E]
                k = lo_ + dma_rr[E]
                dma_rr[E] = (dma_rr[E] + 1) % (hi_ - lo_)
                if dma_last[k] is not None:
                    pj = dma_last[k]
                    if pj not in obs_dma[E] or True:
                        eng.wait_ge(dma_sems[k], dma_cnt[k])
                dma_cnt[k] += 16
                inst = I['fn'](eng)
                inst.then_inc(dma_sems[k], 16)
                sem_of[i] = (dma_sems[k], dma_cnt[k])
                dma_last[k] = i
                continue
            inst = I['fn'](eng)
            if signal[i]:
                if eng_sig[E] >= SEM_CAP:
                    eng_sem[E] = next(sem_iter)
                    eng_sig[E] = 0
                eng_sig[E] += 1
                inst.then_inc(eng_sem[E], 1)
                sem_of[i] = (eng_sem[E], eng_sig[E])
        for k, sm in enumerate(dma_sems):
            if dma_cnt[k] > 0:
                self.eng['sp'].wait_ge(sm, dma_cnt[k])
        self.stats = dict(n=n, waits=nw, per_eng=dict(cnt))
        self.dbg_waits = waits
        self.dbg_seq = seq
        self.dbg_englist = eng_list

import os
import math
from contextlib import ExitStack
from concourse.bass_utils import run_bass_kernel_spmd

F32 = mybir.dt.float32
BF16 = mybir.dt.bfloat16
AF = mybir.ActivationFunctionType
ALU = mybir.AluOpType
AX = mybir.AxisListType

S = 2048
D = 1024
NB = 512
NBLK = 4
DFF = 2816
NFT = 22
RMS_EPS = 1e-6
LN_EPS = 1e-5


def _isnum(x):
    return isinstance(x, (int, float))


class B:
    def __init__(self, nc, P, es):
        self.nc = nc
        self.P = P
        self.es = es
        self.ps = es.enter_context(nc.psum_tensor("ps", [128, 8, 512], F32))
        self._bank = 0
        self._consts = {}
        self.cbuf = es.enter_context(nc.sbuf_tensor("cbuf", [128, 16], F32))
        self._nc = 0

    def sb(self, name, shape, dt, es=None):
        return (es or self.es).enter_context(self.nc.sbuf_tensor(name, shape, dt))

    def bank(self):
        k = self._bank
        self._bank = (k + 1) % 8
        return k

    def const(self, val):
        if val not in self._consts:
            ap = self.cbuf[:, self._nc:self._nc + 1]
            self._nc += 1
            self.memset(ap, float(val), eng='dve')
            self._consts[val] = ap
        return self._consts[val]

    def mm(self, out, lhsT, rhs, start=True, stop=True):
        self.P.op('pe', lambda e: e.matmul(out, lhsT, rhs, start=start, stop=stop), [lhsT, rhs], [out])

    def act(self, out, in_, func, bias=None, scale=1.0):
        r = [in_]
        if func == AF.Copy:
            bias = 0.0
        else:
            if bias is None:
                bias = self.const(0.0)
            elif _isnum(bias):
                bias = self.const(bias)
            npart = out.shape[0]
            if bias.shape[0] != npart:
                per = 1
                for d_ in list(out.tensor.shape)[1:]:
                    per *= d_
                p0 = int(out.offset) // per
                bias = bias[p0:p0 + npart, :]
            r.append(bias)
        if not _isnum(scale):
            r.append(scale)
        self.P.op('act', lambda e: e.activation(out=out, in_=in_, func=func, bias=bias, scale=scale), r, [out])

    def tt(self, out, a, b, op, eng='dve'):
        self.P.op(eng, lambda e: e.tensor_tensor(out=out, in0=a, in1=b, op=op), [a, b], [out])

    def ts(self, out, a, s1, op0, s2=None, op1=None, eng='dve'):
        r = [a]
        if not _isnum(s1):
            r.append(s1)
        if s2 is not None and not _isnum(s2):
            r.append(s2)
        if op1 is None:
            self.P.op(eng, lambda e: e.tensor_scalar(out=out, in0=a, scalar1=s1, scalar2=None, op0=op0), r, [out])
        else:
            self.P.op(eng, lambda e: e.tensor_scalar(out=out, in0=a, scalar1=s1, scalar2=s2, op0=op0, op1=op1), r, [out])

    def stt(self, out, in0, scalar, in1, op0, op1):
        r = [in0, in1]
        if not _isnum(scalar):
            r.append(scalar)
        self.P.op('dve', lambda e: e.scalar_tensor_tensor(out=out, in0=in0, scalar=scalar, in1=in1, op0=op0, op1=op1), r, [out])

    def copy(self, out, in_, eng='dve'):
        if eng == 'act':
            self.act(out, in_, AF.Copy)
        else:
            self.P.op(eng, lambda e: e.tensor_copy(out=out, in_=in_), [in_], [out])

    def memset(self, ap, val, eng='pool'):
        self.P.op(eng, lambda e: e.memset(ap, val), [], [ap])

    def recip(self, out, in_):
        self.P.op('dve', lambda e: e.reciprocal(out=out, in_=in_), [in_], [out])

    def rmax(self, out, in_):
        self.P.op('dve', lambda e: e.reduce_max(out=out, in_=in_, axis=AX.X), [in_], [out])

    def rsum(self, out, in_):
        self.P.op('dve', lambda e: e.reduce_sum(out=out, in_=in_, axis=AX.X), [in_], [out])

    def scan(self, out, d0, d1, init):
        r = [d0, d1]
        if not _isnum(init):
            r.append(init)
        self.P.op('dve', lambda e: e.tensor_tensor_scan(out=out, data0=d0, data1=d1, initial=init, op0=ALU.mult, op1=ALU.add), r, [out])

    def dma(self, out, in_, q='sp'):
        self.P.dma(out, in_, q=q)

    def rstd(self, out, in_, scale, eps, tmp):
        self.act(tmp, in_, AF.Ln, bias=eps, scale=scale)
        self.act(out, tmp, AF.Exp, scale=-0.5)


def wview(w, c0, ncols):
    return w[:, c0:c0 + ncols].rearrange("(kt p) n -> p kt n", p=128)


def build(dbg_stage=None):
    nc = bass.Bass("TRN2", target_bir_lowering=False)

    def din(name, shape):
        return nc.dram_tensor(name, list(shape), F32, kind="ExternalInput").ap()

    xT = din("xT", [D, S])
    ident_d = din("ident", [128, 128])
    L = []
    for l in range(2):
        d = {}
        d['vec'] = din(f"l{l}_vec", [128, 32])
        d['w_out'] = din(f"l{l}_w_out", [D, D])
        d['wgu'] = din(f"l{l}_wgu", [NFT, 128, 8 * 256])
        d['wd'] = din(f"l{l}_wd", [8, 128, NFT * 128])
        L.append(d)
    L[0]['w_in'] = din("l0_w_inx", [D, 3584])
    L[0]['rope'] = din("l0_rope", [128, 2, S])
    L[0]['lam'] = din("l0_lam", [128, 256])
    L[0]['cv'] = din("l0_cv", [128, 16])
    L[0]['dwT'] = din("l0_dwT", [128, 4 * 31])
    L[1]['w_in'] = din("l1_w_inx", [D, 896])
    L[1]['w_uq'] = din("l1_w_uqx", [256, 1536])
    L[1]['w_ukv'] = din("l1_w_ukvx", [128, 1536])
    L[1]['w_glu'] = din("l1_w_glu", [256, 256])
    L[1]['rope'] = din("l1_rope", [128, 2, S])
    L[1]['cv'] = din("l1_cv", [128, 16])
    L[1]['ssm'] = din("l1_ssm", [128, 24])
    L[1]['Bpad'] = din("l1_Bpad", [16, 128, 128])
    L[1]['Cpad'] = din("l1_Cpad", [16, 128, 128])
    L[1]['BpadT'] = din("l1_BpadT", [16, 128, 128])
    outT = nc.dram_tensor("outT", [D, S], F32, kind="ExternalOutput").ap()

    with ExitStack() as es:
        sems = [es.enter_context(nc.semaphore(f"s{i}")) for i in range(80)]
        P = Prog(nc)
        b = B(nc, P, es)
        ps = b.ps
        hT = b.sb("hT", [128, 8, S], F32)
        ones = b.sb("ones", [128, 128], BF16)
        ident = b.sb("ident_sb", [128, 128], F32)
        sq = b.sb("sq", [128, 4, 512], BF16)
        tmpf = b.sb("tmpf", [128, 4, 512], F32)
        osb_ = [None]
        rstd_t = b.sb("rstd_t", [128, 2, 512], F32)
        vec = [b.sb(f"vec{l}", [128, 32], F32) for l in range(2)]
        sqi = [0]
        tfi = [0]

        def sq_next():
            k = sqi[0]
            sqi[0] = (k + 1) % 4
            return sq[:, k, :]

        def tf_next():
            k = tfi[0]
            tfi[0] = (k + 1) % 4
            return tmpf[:, k, :]

        b.memset(ones[:], 1.0)
        b.dma(ident[:], ident_d)
        for l in range(2):
            b.dma(vec[l][:], L[l]['vec'])
        b.dma(hT[:], xT.rearrange("(kt p) s -> p kt s", p=128))

        def blk(i):
            return slice(i * NB, (i + 1) * NB)

        def sumsq_mm(bankS, srcs):
            n = len(srcs)
            for i, sap in enumerate(srcs):
                q = sq_next()
                kp = sap.shape[0]
                b.act(q[0:kp, :], sap, AF.Square)
                b.mm(ps[:, bankS, :], ones[0:kp, :], q[0:kp, :], start=(i == 0), stop=(i == n - 1))

        def prenorm_block(l, goff, bi, tT_dst):
            bankS = b.bank()
            sumsq_mm(bankS, [hT[:, kt, blk(bi)] for kt in range(8)])
            r = rstd_t[:, bi % 2, :]
            b.rstd(r, ps[:, bankS, :], 1.0 / D, RMS_EPS, tf_next())
            for kt in range(8):
                b.stt(tT_dst[:, kt, :], hT[:, kt, blk(bi)], vec[l][:, goff + kt:goff + kt + 1], r, ALU.mult, ALU.mult)

        def postnorm_residual(l, goff, bi, bankS):
            r = rstd_t[:, bi % 2, :]
            b.rstd(r, ps[:, bankS, :], 1.0 / D, RMS_EPS, tf_next())
            for mt in range(8):
                t = tf_next()
                b.stt(t, osb_[0][:, mt, :], vec[l][:, goff + mt:goff + mt + 1], r, ALU.mult, ALU.mult)
                b.tt(hT[:, mt, blk(bi)], hT[:, mt, blk(bi)], t, ALU.add, eng='dve')

        def proj_block(wsb, nkt, rhs_fn, l, goff, bi, lhs_fn=None):
            bankS = b.bank()
            pend = None
            for mt in range(8):
                bk = b.bank()
                if bk == bankS:
                    bk = b.bank()
                for kt in range(nkt):
                    lhsT = lhs_fn(kt, mt) if lhs_fn else wsb[:, kt, mt * 128:(mt + 1) * 128]
                    b.mm(ps[:, bk, :], lhsT, rhs_fn(kt), start=(kt == 0), stop=(kt == nkt - 1))
                if pend is not None:
                    b.mm(ps[:, bankS, :], ones[:], pend[0], start=(pend[1] == 0), stop=False)
                b.act(osb_[0][:, mt, :], ps[:, bk, :], AF.Copy)
                q = sq_next()
                b.act(q, ps[:, bk, :], AF.Square)
                pend = (q, mt)
            b.mm(ps[:, bankS, :], ones[:], pend[0], start=False, stop=True)
            postnorm_residual(l, goff, bi, bankS)

        def ffn(l):
            with ExitStack() as fs:
                tTb = b.sb(f"ffn_tT{l}", [128, 2, 8, 1024], BF16, fs)
                aT = b.sb(f"ffn_aT{l}", [128, NFT, 1024], BF16, fs)
                wgu = b.sb(f"ffn_wgu{l}", [128, 3, 8, 256], BF16, fs)
                wd = b.sb(f"ffn_wd{l}", [128, 3, NFT, 128], BF16, fs)
                sg = b.sb(f"ffn_sg{l}", [128, 2, 512], BF16, fs)
                osb_[0] = b.sb(f"ffn_osb{l}", [128, 8, 512], F32, fs)
                wi = 0
                di = 0
                for bl in range(2):
                    prenorm_block(l, 16, bl, tTb[:, 0, :, bl * 512:(bl + 1) * 512])
                for half in range(2):
                    tT = tTb[:, half]
                    for ft in range(NFT):
                        if half == 0 and ft in (8, 14):
                            bl_ = 0 if ft == 8 else 1
                            prenorm_block(l, 16, 2 + bl_, tTb[:, 1, :, bl_ * 512:(bl_ + 1) * 512])
                        wb = wgu[:, wi % 3]
                        wi += 1
                        b.dma(wb, L[l]['wgu'][ft].rearrange("p (kt c) -> p kt c", c=256), q='pool')
                        for bl in range(2):
                            tk = slice(bl * 512, (bl + 1) * 512)
                            bg = b.bank()
                            bu = b.bank()
                            for kt in range(8):
                                b.mm(ps[:, bg, :], wb[:, kt, 0:128], tT[:, kt, tk], start=(kt == 0), stop=(kt == 7))
                            for kt in range(8):
                                b.mm(ps[:, bu, :], wb[:, kt, 128:256], tT[:, kt, tk], start=(kt == 0), stop=(kt == 7))
                            s_ = sg[:, bl, :]
                            b.act(s_, ps[:, bg, :], AF.Silu)
                            b.tt(aT[:, ft, tk], ps[:, bu, :], s_, ALU.mult)
                    for bl in range(2):
                        bi = half * 2 + bl
                        tk = slice(bl * 512, (bl + 1) * 512)
                        bankS = b.bank()
                        pend = None
                        for mt in range(8):
                            wdb = wd[:, di % 3]
                            di += 1
                            b.dma(wdb, L[l]['wd'][mt].rearrange("p (kt c) -> p kt c", c=128), q='pool')
                            bk = b.bank()
                            if bk == bankS:
                                bk = b.bank()
                            for kt in range(NFT):
                                b.mm(ps[:, bk, :], wdb[:, kt, :], aT[:, kt, tk], start=(kt == 0), stop=(kt == NFT - 1))
                            if pend is not None:
                                b.mm(ps[:, bankS, :], ones[:], pend[0], start=(pend[1] == 0), stop=False)
                            b.act(osb_[0][:, mt, :], ps[:, bk, :], AF.Copy)
                            q = sq_next()
                            b.act(q, ps[:, bk, :], AF.Square)
                            pend = (q, mt)
                        b.mm(ps[:, bankS, :], ones[:], pend[0], start=False, stop=True)
                        postnorm_residual(l, 24, bi, bankS)
            P.barrier()

        S5P = {}

        def s5_params():
            TWO_PI = 2.0 * math.pi
            prm = b.sb("prm", [128, 24], F32)
            sc = b.sb("ssc", [128, 48, 8], F32)
            ki = b.sb("ski", [128, 8], mybir.dt.int32)
            b.dma(prm[:], L[1]['ssm'])
            lr, li, ldt = prm[:, 0:8], prm[:, 8:16], prm[:, 16:24]
            V = lambda i: sc[:, i, :]
            dt_, rr_, th, kf, thp, s4, c4, s2, c2, s1_, c1 = [V(i) for i in range(11)]
            b.act(dt_, ldt, AF.Exp)
            b.tt(V(11), lr, dt_, ALU.mult)
            b.act(rr_, V(11), AF.Exp)
            b.tt(th, li, dt_, ALU.mult)
            b.ts(V(11), th, 1.0 / TWO_PI, ALU.mult)
            b.copy(ki[:], V(11), eng='dve')
            b.copy(kf, ki[:], eng='dve')
            b.stt(thp, kf, -TWO_PI, th, ALU.mult, ALU.add)
            b.act(s4, thp, AF.Sin, scale=0.25)
            b.act(c4, thp, AF.Sin, bias=math.pi / 2, scale=0.25)

            def dbl(so, co, si, ci):
                b.tt(V(11), si, ci, ALU.mult)
                b.ts(so, V(11), 2.0, ALU.mult)
                b.tt(V(12), si, si, ALU.mult)
                b.ts(co, V(12), -2.0, ALU.mult, 1.0, ALU.add)
            dbl(s2, c2, s4, c4)
            dbl(s1_, c1, s2, c2)
            abre, abim, den, nre, fre, fim = [V(i) for i in range(13, 19)]
            b.tt(abre, rr_, c1, ALU.mult)
            b.tt(abim, rr_, s1_, ALU.mult)
            b.tt(V(11), lr, lr, ALU.mult)
            b.tt(V(12), li, li, ALU.mult)
            b.tt(den, V(11), V(12), ALU.add)
            b.recip(den, den)
            b.ts(nre, abre, -1.0, ALU.add)
            b.tt(V(11), nre, lr, ALU.mult)
            b.tt(V(12), abim, li, ALU.mult)
            b.tt(V(11), V(11), V(12), ALU.add)
            b.tt(fre, V(11), den, ALU.mult)
            b.tt(V(11), abim, lr, ALU.mult)
            b.tt(V(12), nre, li, ALU.mult)
            b.tt(V(11), V(11), V(12), ALU.subtract)
            b.tt(fim, V(11), den, ALU.mult)
            def cmul(ore, oim, xr, xi, yr, yi):
                b.tt(V(11), xr, yr, ALU.mult)
                b.tt(V(12), xi, yi, ALU.mult)
                b.tt(V(40), xr, yi, ALU.mult)
                b.tt(V(41), xi, yr, ALU.mult)
                b.tt(ore, V(11), V(12), ALU.subtract)
                b.tt(oim, V(40), V(41), ALU.add)
            PW = b.sb("PW", [128, 16, 8], F32)
            PWr = lambda i: PW[:, i, :]
            b.copy(PWr(0), abre, eng='dve')
            b.copy(PWr(1), abim, eng='dve')
            cmul(PWr(2), PWr(3), abre, abim, abre, abim)
            cmul(PWr(4), PWr(5), PWr(2), PWr(3), abre, abim)
            cmul(PWr(6), PWr(7), PWr(2), PWr(3), PWr(2), PWr(3))
            b.copy(PWr(8), fre, eng='dve')
            b.copy(PWr(9), fim, eng='dve')
            cmul(PWr(10), PWr(11), fre, fim, abre, abim)
            cmul(PWr(12), PWr(13), fre, fim, PWr(2), PWr(3))
            cmul(PWr(14), PWr(15), fre, fim, PWr(4), PWr(5))
            wr, wi_, wr2, wi2 = V(19), V(20), V(21), V(22)
            w4r, w4i = V(44), V(45)
            b.copy(wr, c1, eng='dve')
            b.ts(wi_, s1_, -1.0, ALU.mult)
            r4 = V(42)
            b.tt(V(43), rr_, rr_, ALU.mult)
            b.tt(r4, V(43), V(43), ALU.mult)

            def wsq(cs):
                b.tt(V(11)[:, cs], wr[:, cs], wr[:, cs], ALU.mult)
                b.tt(V(12)[:, cs], wi_[:, cs], wi_[:, cs], ALU.mult)
                b.tt(wr2[:, cs], V(11)[:, cs], V(12)[:, cs], ALU.subtract)
                b.tt(V(11)[:, cs], wr[:, cs], wi_[:, cs], ALU.mult)
                b.ts(wi2[:, cs], V(11)[:, cs], 2.0, ALU.mult)
                b.copy(wr[:, cs], wr2[:, cs], eng='dve')
                b.copy(wi_[:, cs], wi2[:, cs], eng='dve')
            wsq(slice(0, 8))
            wsq(slice(0, 8))

            PWW = b.sb("PWW", [128, 9, 2, 8], F32)
            for lv in range(9):
                b.copy(PWW[:, lv, 0, :], wr, eng='dve')
                b.copy(PWW[:, lv, 1, :], wi_, eng='dve')
                if lv < 8:
                    wsq(slice(0, 8))
            S5P.update(PW=PW, PWW=PWW, r4=r4)

        def attn_pipeline(iters, score_banks, PTbuf, LOOK=2, DEFER=6, early_pre=True):
            deferred = []
            assert len(score_banks) > LOOK
            flat = []
            for it in iters:
                for i in range(it['n']):
                    flat.append((it, i))
            N = len(flat)
            started = set()

            order = {id(it): k for k, it in enumerate(iters)}

            def start_it(it):
                if id(it) not in started:
                    started.add(id(it))
                    if it.get('pre'):
                        it['pre']()

            def emit_qk(g):
                it, i = flat[g]
                start_it(it)
                if early_pre and i == 0 and order[id(it)] + 1 < len(iters):
                    start_it(iters[order[id(it)] + 1])
                it['qk'](i, score_banks[g % len(score_banks)])
            for g in range(min(LOOK, N)):
                emit_qk(g)
            for g in range(N):
                if g + LOOK < N:
                    emit_qk(g + LOOK)
                it, i = flat[g]
                pt = PTbuf[:, g % PTbuf.shape[1], :]
                it['ex'](i, score_banks[g % len(score_banks)], pt)
                it['pv'](i, pt)
                if it.get('bg'):
                    ul = it['bg']()
                    if ul:
                        ul.pop(0)()
                for dfr in sorted([d for d in deferred if d[0] <= g], key=lambda d: d[0]):
                    dfr[1]()
                    deferred.remove(dfr)
                if i == it['n'] - 1:
                    it['post']()
                    if it.get('post_slow'):
                        for (dl, fn) in it['post_slow']:
                            deferred.append((g + dl, fn))
            for dfr in sorted(deferred, key=lambda d: d[0]):
                dfr[1]()

        def dump4(src):
            with ExitStack() as dsk:
                dbgT = b.sb("dbgT", [128, 4, S], F32, dsk)
                for ct in range(4):
                    b.copy(dbgT[:, ct, :], src[:, ct, :], eng='dve')
                b.dma(outT[0:512, :].rearrange("(kt p) s -> p kt s", p=128), dbgT[:])
                P.barrier()

        def layer0_mixer():
            Ld = L[0]
            with ExitStack() as ms:
                cv = b.sb("cv", [128, 16], F32, ms)
                b.dma(cv[:], Ld['cv'])
                yaT = b.sb("yaT", [128, 4, S], BF16, ms)
                qs = ExitStack()
                qrT = b.sb("qrT", [128, 4, S], BF16, qs)
                krT = b.sb("krT", [128, 2, 4, S], BF16, qs)
                b.memset(krT[64:128, 0], 0.0)
                b.memset(krT[0:64, 1], 0.0)
                vtm = b.sb("vtm", [128, 16, 512], BF16, qs)
                with ExitStack() as s1:
                    tTh = b.sb("tTh2", [128, 8, 1024], BF16, s1)
                    wbuf = b.sb("wbuf2", [128, 3, 8, 256], BF16, s1)
                    rope = b.sb("rope", [128, 2, 1024], F32, s1)
                    wi = 0
                    for half in range(2):
                        b.dma(rope[:], Ld['rope'][:, :, half * 1024:(half + 1) * 1024])
                        for bl in range(2):
                            prenorm_block(0, 0, half * 2 + bl, tTh[:, :, bl * 512:(bl + 1) * 512])
                        for g in range(8):
                            wb = wbuf[:, wi % 3]
                            wi += 1
                            b.dma(wb, wview(Ld['w_in'], g * 256, 256), q='pool')
                            for bl in range(2):
                                bi = half * 2 + bl
                                bA = b.bank()
                                bB = b.bank()
                                for kt in range(8):
                                    b.mm(ps[:, bA, :], wb[:, kt, 0:128], tTh[:, kt, bl * 512:(bl + 1) * 512], start=(kt == 0), stop=(kt == 7))
                                for kt in range(8):
                                    b.mm(ps[:, bB, :], wb[:, kt, 128:256], tTh[:, kt, bl * 512:(bl + 1) * 512], start=(kt == 0), stop=(kt == 7))
                                t1 = tf_next()
                                t2 = tf_next()
                                b.tt(t1, ps[:, bA, :], rope[:, 0, bl * 512:(bl + 1) * 512], ALU.mult)
                                b.tt(t2, ps[:, bB, :], rope[:, 1, bl * 512:(bl + 1) * 512], ALU.mult)
                                if g < 4:
                                    b.tt(qrT[:, g % 4, blk(bi)], t1, t2, ALU.add, eng='dve')
                                else:
                                    b.tt(krT[0:64, 0, g % 4, blk(bi)], t1[0:64, :], t2[0:64, :], ALU.add, eng='dve')
                                    b.tt(krT[64:128, 1, g % 4, blk(bi)], t1[64:128, :], t2[64:128, :], ALU.add, eng='dve')
                        for g in range(2):
                            wb = wbuf[:, wi % 3]
                            wi += 1
                            b.dma(wb, wview(Ld['w_in'], 3072 + g * 256, 256), q='pool')
                            for tt_ in range(8):
                                bk = b.bank()
                                tok = slice(tt_ * 128, (tt_ + 1) * 128)
                                for kt in range(8):
                                    b.mm(ps[:, bk, 0:256], tTh[:, kt, tok], wb[:, kt, :], start=(kt == 0), stop=(kt == 7))
                                b.copy(vtm[:, half * 8 + tt_, g * 256:(g + 1) * 256], ps[:, bk, 0:256], eng=('act' if tt_ % 2 else 'dve'))
                P.barrier()
                with ExitStack() as s3:
                    PT = b.sb("PT", [128, 4, 512], BF16, s3)
                    lam = b.sb("lam", [128, 256], F32, s3)
                    lsc = b.sb("lsc", [128, 16], F32, s3)
                    mx = b.sb("mx", [128, 16], F32, s3)
                    negM = b.sb("negM", [128, 4], F32, s3)
                    b.dma(lam[:], Ld['lam'])
                    b.tt(lam[:, 0:64], lam[:, 0:64], lam[:, 64:128], ALU.mult)
                    b.tt(lam[:, 128:192], lam[:, 128:192], lam[:, 192:256], ALU.mult)
                    b.rsum(lsc[:, 0:1], lam[:, 0:64])
                    b.rsum(lsc[:, 1:2], lam[:, 128:192])
                    b.act(lsc[:, 2:4], lsc[:, 0:2], AF.Exp)
                    b.ts(lsc[:, 4:5], lsc[:, 3:4], -0.2, ALU.add)
                    b.tt(lsc[:, 5:6], lsc[:, 4:5], lsc[:, 2:3], ALU.subtract)
                    neglam = lsc[:, 5:6]
                    b.ts(lsc[:, 6:7], cv[:, 0:1], 0.8, ALU.mult)
                    sub8 = lsc[:, 6:7]
                    for h in range(4):
                        for wh in range(2):
                            for bi in range(NBLK):
                                bk = b.bank()
                                q = sq_next()
                                if wh == 0:
                                    b.act(q, qrT[:, h, blk(bi)], AF.Square)
                                    b.mm(ps[:, bk, :], ones[:], q)
                                else:
                                    q2 = sq_next()
                                    b.act(q, krT[:, 0, h, blk(bi)], AF.Square)
                                    b.act(q2, krT[:, 1, h, blk(bi)], AF.Square)
                                    b.mm(ps[:, bk, :], ones[:], q, start=True, stop=False)
                                    b.mm(ps[:, bk, :], ones[:], q2, start=False, stop=True)
                                b.rmax(mx[:, wh * 4 + bi:wh * 4 + bi + 1], ps[:, bk, :])
                            b.rmax(mx[:, 8 + wh:9 + wh], mx[:, wh * 4:wh * 4 + 4])
                        b.tt(mx[:, 10:11], mx[:, 8:9], mx[:, 9:10], ALU.mult)
                        b.act(mx[:, 11:12], mx[:, 10:11], AF.Sqrt)
                        b.ts(negM[:, h:h + 1], mx[:, 11:12], -0.125 * 1.02, ALU.mult)
                    evO = b.sb("evO", [128, 1, 2, 512], F32, s3)
                    d2b = b.sb("d2b", [128, 2, 512], F32, s3)
                    sqd = b.sb("sqd", [128, 2, 512], BF16, s3)
                    r0b = b.sb("r0b", [128, 2, 512], F32, s3)
                    r1b = b.sb("r1b", [128, 1, 512], F32, s3)
                    evS = b.sb("evS", [128, 1, 2, 512], F32, s3)
                    iters = []
                    for h in range(4):
                        for qb in range(NBLK):
                            def mk(h=h, qb=qb, idx=len(iters)):
                                nkt = 4 * qb + 4
                                bO = [0, 1]
                                bSm = [2, 3]

                                trim = qb > 0
                                korder = [4 * qb + d for d in range(4)] + list(range(4 * qb))

                                def cols(kt):
                                    d_ = kt - 4 * qb
                                    return 128 * d_ if (trim and d_ > 0) else 0

                                def qk(i, bank):
                                    kt, c = korder[i // 2], i % 2
                                    c0 = cols(kt)
                                    b.mm(ps[:, bank, c0:512], krT[:, c, h, kt * 128:(kt + 1) * 128], qrT[:, h, qb * 512 + c0:(qb + 1) * 512])

                                def ex(i, bank, pt):
                                    kt = korder[i // 2]
                                    d_ = kt - 4 * qb
                                    c0 = cols(kt)
                                    if d_ <= 0 or trim:
                                        b.act(pt[:, c0:], ps[:, bank, c0:], AF.Exp, bias=negM[:, h:h + 1], scale=0.125)
                                    else:
                                        b.memset(pt[:, 0:128 * d_], 0.0)
                                        b.act(pt[:, 128 * d_:], ps[:, bank, 128 * d_:], AF.Exp, bias=negM[:, h:h + 1], scale=0.125)
                                    if d_ >= 0:
                                        b.memset(pt[64:128, 128 * d_:128 * d_ + 64], 0.0)

                                def pv(i, pt):
                                    j_ = i // 2
                                    kt, c = korder[j_], i % 2
                                    c0 = cols(kt)
                                    b.mm(ps[:, bO[c], c0:512], vtm[:, kt, h * 128:(h + 1) * 128], pt[:, c0:], start=(j_ == 0), stop=(j_ == nkt - 1))
                                    b.mm(ps[:, bSm[c], c0:512], ones[:], pt[:, c0:], start=(j_ == 0), stop=(j_ == nkt - 1))

                                def post():
                                    e = 0
                                    for c in range(2):
                                        b.copy(evO[:, e, c, :], ps[:, bO[c], :], eng='dve')
                                        b.copy(evS[:, e, c, :], ps[:, bSm[c], :], eng='dve')
                                    for c in range(2):
                                        b.recip(evS[:, e, c, :], evS[:, e, c, :])
                                        b.tt(evO[:, e, c, :], evO[:, e, c, :], evS[:, e, c, :], ALU.mult)

                                def stA():
                                    b.stt(d2b[:, idx % 2, :], evO[:, 0, 1, :], neglam, evO[:, 0, 0, :], ALU.mult, ALU.add)
                                    b.tt(sqd[:, idx % 2, :], d2b[:, idx % 2, :], d2b[:, idx % 2, :], ALU.mult, eng='pool')

                                def stB():
                                    b.mm(ps[:, 7, :], ones[:], sqd[:, idx % 2, :])

                                def stC():
                                    b.rstd(r0b[:, idx % 2, :], ps[:, 7, :], 1.0 / 128, RMS_EPS, r1b[:, 0, :])

                                def stD():
                                    b.stt(yaT[:, h, blk(qb)], d2b[:, idx % 2, :], sub8, r0b[:, idx % 2, :], ALU.mult, ALU.mult)
                                post_slow = [(6, stA), (10, stB), (13, stC), (16, stD)]
                                return dict(n=2 * nkt, qk=qk, ex=ex, pv=pv, post=post, post_slow=post_slow)
                            iters.append(mk())
                    attn_pipeline(iters, [4, 5, 6], PT)
                qs.close()
                P.barrier()
                if dbg_stage == 'ya':
                    dump4(yaT)
                    return
                ybT = b.sb("ybT", [128, 4, S], BF16, ms)
                with ExitStack() as su:
                    upad = b.sb("upad", [128, 4, 30 + S], BF16, su)
                    b.memset(upad[:, :, 0:30], 0.0)
                    with ExitStack() as s1:
                        tTh = b.sb("tTh", [128, 8, 1024], BF16, s1)
                        wbuf = b.sb("wbuf", [128, 3, 8, 256], BF16, s1)
                        sgt = b.sb("sgt", [128, 2, 512], F32, s1)
                        wi = 0
                        for half in range(2):
                            for bl in range(2):
                                prenorm_block(0, 0, half * 2 + bl, tTh[:, :, bl * 512:(bl + 1) * 512])
                            for i in range(4):
                                wb = wbuf[:, wi % 3]
                                wi += 1
                                b.dma(wb, wview(Ld['w_in'], (8 + i) * 256, 256), q='pool')
                                for bl in range(2):
                                    bi = half * 2 + bl
                                    bA = b.bank()
                                    bB = b.bank()
                                    for kt in range(8):
                                        b.mm(ps[:, bA, :], wb[:, kt, 0:128], tTh[:, kt, bl * 512:(bl + 1) * 512], start=(kt == 0), stop=(kt == 7))
                                    for kt in range(8):
                                        b.mm(ps[:, bB, :], wb[:, kt, 128:256], tTh[:, kt, bl * 512:(bl + 1) * 512], start=(kt == 0), stop=(kt == 7))
                                    s_ = sgt[:, bl, :]
                                    b.act(s_, ps[:, bB, :], AF.Sigmoid)
                                    b.tt(upad[:, i, 30 + bi * 512:30 + (bi + 1) * 512], ps[:, bA, :], s_, ALU.mult)
                    P.barrier()
                    with ExitStack() as s2:
                        diag = b.sb("diag", [128, 124, 128], BF16, s2)
                        dwT = b.sb("dwT", [128, 124], F32, s2)
                        ysb = b.sb("ysb", [128, 4, 512], F32, s2)
                        ybf = b.sb("ybf", [128, 4, 512], BF16, s2)
                        st = b.sb("cst", [128, 4, 512], F32, s2)
                        b.dma(dwT[:], Ld['dwT'])
                        for j in range(124):
                            b.ts(diag[:, j, :], ident[:], dwT[:, j:j + 1], ALU.mult, eng='dve')
                        for bi in range(NBLK):
                            bM = b.bank()
                            bS = b.bank()
                            for ct in range(4):
                                bk = b.bank()
                                while bk in (bM, bS):
                                    bk = b.bank()
                                for j in range(31):
                                    b.mm(ps[:, bk, :], diag[:, ct * 31 + j, :], upad[:, ct, bi * 512 + j:bi * 512 + j + 512], start=(j == 0), stop=(j == 30))
                                b.act(ysb[:, ct, :], ps[:, bk, :], AF.Identity, bias=cv[:, 1 + ct:2 + ct])
                                b.copy(ybf[:, ct, :], ysb[:, ct, :], eng='dve')
                                q = sq_next()
                                b.act(q, ysb[:, ct, :], AF.Square)
                                b.mm(ps[:, bM, :], ones[:], ybf[:, ct, :], start=(ct == 0), stop=(ct == 3))
                                b.mm(ps[:, bS, :], ones[:], q, start=(ct == 0), stop=(ct == 3))
                            mean = st[:, 0, :]
                            m2 = st[:, 1, :]
                            var = st[:, 2, :]
                            rs = st[:, 3, :]
                            b.act(mean, ps[:, bM, :], AF.Copy, scale=1.0 / 512)
                            b.tt(m2, mean, mean, ALU.mult)
                            b.stt(var, ps[:, bS, :], 1.0 / 512, m2, ALU.mult, ALU.subtract)
                            b.rstd(rs, var, 1.0, LN_EPS, m2)
                            for ct in range(4):
                                t1 = tf_next()
                                b.tt(t1, ysb[:, ct, :], mean, ALU.subtract)
                                b.tt(t1, t1, rs, ALU.mult)
                                b.act(ybT[:, ct, blk(bi)], t1, AF.Silu, bias=cv[:, 9 + ct:10 + ct], scale=cv[:, 5 + ct:6 + ct])
                P.barrier()
                if dbg_stage == 'yb':
                    dump4(ybT)
                    return
                with ExitStack() as s4:
                    wo = b.sb("wo0", [128, 8, D], BF16, s4)
                    osb_[0] = b.sb("osb0", [128, 8, 512], F32, s4)
                    b.dma(wo[:, 0:4], wview(Ld['w_out'], 0, D)[:, 0:4], q='pool')
                    b.dma(wo[:, 4:8], wview(Ld['w_out'], 0, D)[:, 4:8], q='pool')
                    for bi in range(NBLK):
                        proj_block(wo, 8, lambda kt: (yaT if kt < 4 else ybT)[:, kt % 4, blk(bi)], 0, 8, bi)
            P.barrier()


        def dumpN(src, n):
            with ExitStack() as dsk:
                dbgT = b.sb("dbgT1", [128, n, S], F32, dsk)
                for ct in range(n):
                    b.copy(dbgT[:, ct, :], src[:, ct, :], eng='dve')
                b.dma(outT[0:128 * n, :].rearrange("(kt p) s -> p kt s", p=128), dbgT[:])
                P.barrier()

        def layer1_mixer():
            Ld = L[1]
            TWO_PI = 2.0 * math.pi
            with ExitStack() as ms:
                cv = b.sb("cv1", [128, 16], F32, ms)
                b.dma(cv[:], Ld['cv'])
                ycT = b.sb("ycT", [128, 2, S], BF16, ms)
                cqn = b.sb("cqn", [128, 2, S], BF16, ms)
                ckvn = b.sb("ckvn", [128, S], BF16, ms)
                krope = b.sb("krope", [128, 2, S], BF16, ms)
                b.memset(krope[64:128, 0], 0.0)
                b.memset(krope[0:64, 1], 0.0)
                with ExitStack() as ss:
                    uT = b.sb("uT", [128, 2, S], BF16, ss)
                    with ExitStack() as s1:
                        tTh = b.sb("tTh3", [128, 8, 1024], BF16, s1)
                        wb = b.sb("wbuf3", [128, 8, 896], BF16, s1)
                        rope = b.sb("rope1a", [128, 2, 1024], F32, s1)
                        cqf = b.sb("cqf", [128, 3, 512], F32, s1)
                        b.dma(wb[:, :, 0:512], wview(Ld['w_in'], 0, 512), q='pool')
                        b.dma(wb[:, :, 512:896], wview(Ld['w_in'], 512, 384), q='pool')
                        for half in range(2):
                            b.dma(rope[:], Ld['rope'][:, :, half * 1024:(half + 1) * 1024])
                            for bl in range(2):
                                prenorm_block(1, 0, half * 2 + bl, tTh[:, :, bl * 512:(bl + 1) * 512])
                            for bl in range(2):
                                bi = half * 2 + bl
                                tk = slice(bl * 512, (bl + 1) * 512)
                                banks = []
                                for m in range(7):
                                    bk = b.bank()
                                    banks.append(bk)
                                    for kt in range(8):
                                        b.mm(ps[:, bk, :], wb[:, kt, m * 128:(m + 1) * 128], tTh[:, kt, tk], start=(kt == 0), stop=(kt == 7))
                                    if m < 2:
                                        b.copy(uT[:, m, blk(bi)], ps[:, bk, :], eng='act')
                                    elif m < 5:
                                        b.copy(cqf[:, m - 2, :], ps[:, bk, :], eng=('act' if m % 2 else 'dve'))
                                bS = b.bank()
                                sumsq_mm(bS, [cqf[:, 0, :], cqf[:, 1, :]])
                                r = tf_next()
                                b.rstd(r, ps[:, bS, :], 1.0 / 256, RMS_EPS, tf_next())
                                for m in range(2):
                                    b.stt(cqn[:, m, blk(bi)], cqf[:, m, :], cv[:, m:m + 1], r, ALU.mult, ALU.mult)
                                bS = b.bank()
                                sumsq_mm(bS, [cqf[:, 2, :]])
                                r = tf_next()
                                b.rstd(r, ps[:, bS, :], 1.0 / 128, RMS_EPS, tf_next())
                                b.stt(ckvn[:, blk(bi)], cqf[:, 2, :], cv[:, 2:3], r, ALU.mult, ALU.mult)
                                t1 = tf_next()
                                t2 = tf_next()
                                b.tt(t1, ps[:, banks[5], :], rope[:, 0, tk], ALU.mult)
                                b.tt(t2, ps[:, banks[6], :], rope[:, 1, tk], ALU.mult)
                                b.tt(krope[0:64, 0, blk(bi)], t1[0:64, :], t2[0:64, :], ALU.add, eng='dve')
                                b.tt(krope[64:128, 1, blk(bi)], t1[64:128, :], t2[64:128, :], ALU.add, eng='dve')
                    P.barrier()
                    Cq = b.sb("Cq", [128, 16, 128], BF16, ss)
                    b.dma(Cq[:], Ld['Cpad'].rearrange("a r c -> r a c"), q='pool')
                    PW, PWW, r4 = S5P['PW'], S5P['PWW'], S5P['r4']
                    KdT = b.sb("KdT", [128, 4, 2, 128], BF16, ss)
                    ysb = b.sb("ysb1", [128, 2, S], F32, ss)
                    with ExitStack() as st_:
                        BTf = b.sb("BTf", [128, 2, 2, 128], F32, st_)
                        Ldb = b.sb("Ldb", [128, 2, 2, 4, 128], BF16, st_)
                        u4 = b.sb("u4", [128, 2, 4, 128], F32, st_)
                        for ct in range(2):
                            bks = [b.bank() for _ in range(4)]
                            for jj in range(4):
                                j = ct * 4 + jj
                                bt = BTf[:, jj % 2]
                                b.dma(bt[:, 0, :], Ld['BpadT'][j], q='sp')
                                b.dma(bt[:, 1, :], Ld['BpadT'][8 + j], q='sp')
                                btr = bt[:, 0, :].unsqueeze(1).to_broadcast([128, 4, 128])
                                bti = bt[:, 1, :].unsqueeze(1).to_broadcast([128, 4, 128])
                                hre = PW[:, 8:16:2, j:j + 1].to_broadcast([128, 4, 128])
                                him = PW[:, 9:16:2, j:j + 1].to_broadcast([128, 4, 128])
                                lb = Ldb[:, jj % 2]
                                b.tt(u4[:, 0], btr, hre, ALU.mult)
                                b.tt(u4[:, 1], bti, him, ALU.mult)
                                b.tt(lb[:, 0], u4[:, 0], u4[:, 1], ALU.subtract)
                                b.tt(u4[:, 0], btr, him, ALU.mult)
                                b.tt(u4[:, 1], bti, hre, ALU.mult)
                                b.stt(lb[:, 1], u4[:, 0], -1.0, u4[:, 1], ALU.mult, ALU.subtract)
                                for d_ in range(4):
                                    reg = ps[:, bks[d_], 0:128]
                                    b.mm(reg, lb[:, 0, d_, :], Cq[:, j, :], start=(jj == 0), stop=False)
                                    b.mm(reg, lb[:, 1, d_, :], Cq[:, 8 + j, :], start=False, stop=(jj == 3))
                            b.stt(KdT[:, 0, ct, :], ident[:], cv[:, 3 + ct:4 + ct], ps[:, bks[0], 0:128], ALU.mult, ALU.add)
                            for d_ in range(1, 4):
                                b.copy(KdT[:, d_, ct, :], ps[:, bks[d_], 0:128], eng='act')
                    P.barrier()
                    for ct in range(2):
                        with ExitStack() as sct:
                            c4s = slice(ct * 4, ct * 4 + 4)
                            Rre = b.sb(f"Rre{ct}", [128, 4, 512], F32, sct)
                            Rim = b.sb(f"Rim{ct}", [128, 4, 512], F32, sct)
                            BH = b.sb(f"BH{ct}", [128, 4, 4, 2, 128], BF16, sct)
                            Cs = b.sb(f"Cs{ct}", [128, 4, 4, 2, 128], BF16, sct)
                            cz = b.sb(f"cz{ct}", [128, 1, 2, 512], F32, sct)
                            xx = b.sb(f"xx{ct}", [128, 2, 2, 516], BF16, sct)
                            b.memset(xx[:, :, :, 0:1], 0.0)
                            b.memset(Rre[:, :, 0:1], 1.0, eng='dve')
                            b.memset(Rim[:, :, 0:1], 0.0, eng='dve')
                            with ExitStack() as st_:
                                tA = b.sb(f"tabA{ct}", [128, 4, 256], F32, st_)
                                tB = b.sb(f"tabB{ct}", [128, 4, 256], F32, st_)
                                m = 1
                                lv = 0
                                while m < 512:
                                    wrb = PWW[:, lv, 0, c4s].unsqueeze(2).to_broadcast([128, 4, m])
                                    wib = PWW[:, lv, 1, c4s].unsqueeze(2).to_broadcast([128, 4, m])
                                    a_, c_ = tA[:, :, 0:m], tB[:, :, 0:m]
                                    b.tt(a_, Rre[:, :, 0:m], wrb, ALU.mult)
                                    b.tt(c_, Rim[:, :, 0:m], wib, ALU.mult)
                                    b.tt(Rre[:, :, m:2 * m], a_, c_, ALU.subtract)
                                    b.tt(a_, Rre[:, :, 0:m], wib, ALU.mult)
                                    b.tt(c_, Rim[:, :, 0:m], wrb, ALU.mult)
                                    b.tt(Rim[:, :, m:2 * m], a_, c_, ALU.add)
                                    m *= 2
                                    lv += 1
                                Bf = b.sb(f"Bf{ct}", [128, 2, 2, 128], F32, st_)
                                Cf = b.sb(f"Cf{ct}", [128, 2, 2, 128], F32, st_)
                                dg = b.sb(f"dgf{ct}", [128, 8, 128], F32, st_)
                                dgh = b.sb(f"dgh{ct}", [128, 8, 128], BF16, st_)
                                dgl = b.sb(f"dgl{ct}", [128, 8, 128], BF16, st_)
                                hrow = b.sb(f"hrow{ct}", [128, 8, 128], F32, st_)
                                t4a = tA[:, 0:2, :].rearrange("p a (c n) -> p (a c) n", n=128)
                                t4b = tB[:, 0:2, :].rearrange("p a (c n) -> p (a c) n", n=128)
                                hr4 = hrow[:].rearrange("p (d c) n -> p d c n", c=2)
                                idb = ident[:].unsqueeze(1).to_broadcast([128, 8, 128])
                                for jj in range(4):
                                    j = ct * 4 + jj
                                    Bfj, Cfj = Bf[:, jj % 2], Cf[:, jj % 2]
                                    b.dma(Bfj[:, 0, :], Ld['Bpad'][j], q='sp')
                                    b.dma(Bfj[:, 1, :], Ld['Bpad'][8 + j], q='sp')
                                    b.dma(Cfj[:, 0, :], Ld['Cpad'][j], q='sp')
                                    b.dma(Cfj[:, 1, :], Ld['Cpad'][8 + j], q='sp')
                                    b.tt(dg[:], idb, PW[:, 8:16, j:j + 1].to_broadcast([128, 8, 128]), ALU.mult)
                                    b.copy(dgh[:], dg[:], eng='dve')
                                    b.tt(dgl[:], dg[:], dgh[:], ALU.subtract)
                                    for half in range(2):
                                        bk = b.bank()
                                        while bk >= 4:
                                            bk = b.bank()
                                        b.mm(ps[:, bk, :], ones[:], dgh[:, 4 * half:4 * half + 4, :], start=True, stop=False)
                                        b.mm(ps[:, bk, :], ones[:], dgl[:, 4 * half:4 * half + 4, :], start=False, stop=True)
                                        b.copy(hrow[:, 4 * half:4 * half + 4, :], ps[:, bk, :].rearrange("p (a n) -> p a n", n=128), eng='act')
                                    bre_b = Bfj[:, 0, :].unsqueeze(1).to_broadcast([128, 4, 128])
                                    bim_b = Bfj[:, 1, :].unsqueeze(1).to_broadcast([128, 4, 128])
                                    b.tt(t4a, bre_b, hr4[:, :, 0, :], ALU.mult)
                                    b.tt(t4b, bim_b, hr4[:, :, 1, :], ALU.mult)
                                    b.tt(BH[:, jj, :, 0, :], t4a, t4b, ALU.subtract)
                                    b.tt(t4a, bre_b, hr4[:, :, 1, :], ALU.mult)
                                    b.tt(t4b, bim_b, hr4[:, :, 0, :], ALU.mult)
                                    b.tt(BH[:, jj, :, 1, :], t4a, t4b, ALU.add)
                                    cre_b = Cfj[:, 0, :].unsqueeze(1).to_broadcast([128, 4, 128])
                                    cim_b = Cfj[:, 1, :].unsqueeze(1).to_broadcast([128, 4, 128])
                                    prb = PW[:, 0:8:2, j:j + 1].to_broadcast([128, 4, 128])
                                    pib = PW[:, 1:8:2, j:j + 1].to_broadcast([128, 4, 128])
                                    t4c = tA[:, 2:4, :].rearrange("p a (c n) -> p (a c) n", n=128)
                                    t4d = tB[:, 2:4, :].rearrange("p a (c n) -> p (a c) n", n=128)
                                    b.tt(t4c, cre_b, prb, ALU.mult, eng='pool')
                                    b.tt(t4d, cim_b, pib, ALU.mult, eng='pool')
                                    b.tt(Cs[:, jj, :, 0, :], t4c, t4d, ALU.subtract, eng='pool')
                                    b.tt(t4c, cre_b, pib, ALU.mult, eng='pool')
                                    b.tt(t4d, cim_b, prb, ALU.mult, eng='pool')
                                    b.tt(Cs[:, jj, :, 1, :], t4c, t4d, ALU.add, eng='pool')
                            for s_ in range(4):
                                for sp_ in range(s_ + 1):
                                    b.mm(ps[:, 4 + s_, :], KdT[:, s_ - sp_, ct, :], uT[:, ct, sp_:S:4], start=(sp_ == 0), stop=False)
                            for jj in range(4):
                                j = ct * 4 + jj
                                k_ = jj % 2
                                bre, bim = k_ * 2, k_ * 2 + 1
                                for s_ in range(4):
                                    b.mm(ps[:, bre, :], BH[:, jj, 3 - s_, 0, :], uT[:, ct, s_:S:4], start=(s_ == 0), stop=(s_ == 3))
                                for s_ in range(4):
                                    b.mm(ps[:, bim, :], BH[:, jj, 3 - s_, 1, :], uT[:, ct, s_:S:4], start=(s_ == 0), stop=(s_ == 3))
                                t1, t2, t3, t4 = tf_next(), tf_next(), tf_next(), tf_next()
                                b.tt(t1, ps[:, bre, :], Rre[:, jj, :], ALU.mult)
                                b.tt(t2, ps[:, bim, :], Rim[:, jj, :], ALU.mult)
                                b.tt(t3, ps[:, bim, :], Rre[:, jj, :], ALU.mult)
                                b.tt(t4, ps[:, bre, :], Rim[:, jj, :], ALU.mult)
                                b.tt(cz[:, 0, 0, :], t1, t2, ALU.subtract, eng='pool')
                                b.tt(cz[:, 0, 1, :], t3, t4, ALU.add, eng='pool')
                                rb = r4[:, j:j + 1].to_broadcast([128, 512])
                                b.scan(cz[:, 0, 0, :], rb, cz[:, 0, 0, :], 0.0)
                                b.scan(cz[:, 0, 1, :], rb, cz[:, 0, 1, :], 0.0)
                                t1, t2, t3, t4 = tf_next(), tf_next(), tf_next(), tf_next()
                                b.tt(t1, cz[:, 0, 0, :], Rre[:, jj, :], ALU.mult)
                                b.tt(t2, cz[:, 0, 1, :], Rim[:, jj, :], ALU.mult)
                                b.tt(t3, cz[:, 0, 0, :], Rim[:, jj, :], ALU.mult)
                                b.tt(t4, cz[:, 0, 1, :], Rre[:, jj, :], ALU.mult)
                                b.tt(xx[:, k_, 0, 1:513], t1, t2, ALU.add, eng='pool')
                                b.tt(xx[:, k_, 1, 1:513], t3, t4, ALU.subtract, eng='pool')
                                for s_ in range(4):
                                    b.mm(ps[:, 4 + s_, :], Cs[:, jj, s_, 0, :], xx[:, k_, 0, 0:512], start=False, stop=False)
                                    b.mm(ps[:, 4 + s_, :], Cs[:, jj, s_, 1, :], xx[:, k_, 1, 0:512], start=False, stop=(jj == 3))
                            for s_ in range(4):
                                b.copy(ysb[:, ct, s_:S:4], ps[:, 4 + s_, :], eng=('act' if s_ % 2 else 'dve'))
                        P.barrier()
                    wg = b.sb("wglu", [128, 2, 256], BF16, ss)
                    b.dma(wg[:], wview(Ld['w_glu'], 0, 256), q='pool')
                    zbf = b.sb("zbf", [128, 2, S], BF16, ss)
                    for bi in range(NBLK):
                        for ct in range(2):
                            y_ = ysb[:, ct, blk(bi)]
                            t1, t2 = tf_next(), tf_next()
                            b.tt(t1, y_, y_, ALU.mult)
                            b.ts(t1, t1, 0.044715, ALU.mult, 1.0, ALU.add)
                            b.tt(t1, t1, y_, ALU.mult)
                            b.act(t2, t1, AF.Sigmoid, scale=1.5957691216057308)
                            b.tt(y_, y_, t2, ALU.mult)
                            b.copy(zbf[:, ct, blk(bi)], y_, eng='dve')
                        for mt in range(2):
                            bk = b.bank()
                            for kt in range(2):
                                b.mm(ps[:, bk, :], wg[:, kt, mt * 128:(mt + 1) * 128], zbf[:, kt, blk(bi)], start=(kt == 0), stop=(kt == 1))
                            t2 = tf_next()
                            b.act(t2, ps[:, bk, :], AF.Sigmoid, bias=cv[:, 5 + mt:6 + mt])
                            b.tt(ycT[:, mt, blk(bi)], ysb[:, mt, blk(bi)], t2, ALU.mult)
                P.barrier()
                if dbg_stage == 'yc':
                    dumpN(ycT, 2)
                    return
                ydT = b.sb("ydT", [128, 6, S], BF16, ms)
                qrT = b.sb("qrT1", [128, 3, S], BF16, ms)
                wuq = b.sb("wuq", [128, 2, 1536], BF16, ms)
                wukv = b.sb("wukv", [128, 1536], BF16, ms)
                b.dma(wuq[:], wview(Ld['w_uq'], 0, 1536), q='pool')
                b.dma(wukv[:], Ld['w_ukv'], q='pool')
                with ExitStack() as s1:
                    rope = b.sb("rope1b", [128, 2, S], F32, s1)
                    b.dma(rope[:], Ld['rope'])
                    for pr in range(3):
                        for bi in range(NBLK):
                            bA, bB = b.bank(), b.bank()
                            for kt in range(2):
                                b.mm(ps[:, bA, :], wuq[:, kt, 768 + pr * 128:768 + (pr + 1) * 128], cqn[:, kt, blk(bi)], start=(kt == 0), stop=(kt == 1))
                            for kt in range(2):
                                b.mm(ps[:, bB, :], wuq[:, kt, 1152 + pr * 128:1152 + (pr + 1) * 128], cqn[:, kt, blk(bi)], start=(kt == 0), stop=(kt == 1))
                            t1, t2 = tf_next(), tf_next()
                            b.tt(t1, ps[:, bA, :], rope[:, 0, blk(bi)], ALU.mult)
                            b.tt(t2, ps[:, bB, :], rope[:, 1, blk(bi)], ALU.mult)
                            b.tt(qrT[:, pr, blk(bi)], t1, t2, ALU.add, eng='dve')
                P.barrier()
                with ExitStack() as s3:
                    PT = b.sb("PT1", [128, 4, 512], BF16, s3)
                    mx = b.sb("mx1", [128, 2, 16], F32, s3)
                    negM = b.sb("negM1", [128, 6], F32, s3)
                    rr = b.sb("rr1", [128, 2, 512], F32, s3)
                    qnT = b.sb("qnT", [128, 2, S], BF16, s3)
                    knT = b.sb("knT", [128, 2, S], BF16, s3)
                    vt = b.sb("vt1", [128, 2, 16, 128], BF16, s3)
                    SC = 192.0 ** -0.5
                    evO = b.sb("evO1", [128, 2, 512], F32, s3)
                    evS = b.sb("evS1", [128, 2, 512], F32, s3)
                    iters = []
                    units_of = {}
                    sqm = b.sb("sqm", [128, 2, 2, 512], BF16, s3)
                    for h in range(6):
                        hb = h % 2
                        hp = slice(64 * (h % 2), 64 * (h % 2) + 64)

                        def make_units(h=h, hb=hb, hp=hp):
                            U = []
                            pb = [7, 7]
                            cnt = [0]

                            def nb():
                                cnt[0] += 1
                                return pb[cnt[0] % 2]
                            for bi in range(NBLK):
                                def uq(bi=bi):
                                    bk = nb()
                                    b.mm(ps[:, bk, :], wuq[:, 0, h * 128:(h + 1) * 128], cqn[:, 0, blk(bi)], start=True, stop=False)
                                    b.mm(ps[:, bk, :], wuq[:, 1, h * 128:(h + 1) * 128], cqn[:, 1, blk(bi)], start=False, stop=True)
                                    b.copy(qnT[:, hb, blk(bi)], ps[:, bk, :], eng='dve')
                                U.append(uq)

                                def uk(bi=bi):
                                    bk = nb()
                                    b.mm(ps[:, bk, :], wukv[:, h * 128:(h + 1) * 128], ckvn[:, blk(bi)])
                                    b.copy(knT[:, hb, blk(bi)], ps[:, bk, :], eng='dve')
                                U.append(uk)
                            for t4_ in range(4):
                                def uv(t4_=t4_):
                                    bk = nb()
                                    for u_ in range(4):
                                        tti = t4_ * 4 + u_
                                        b.mm(ps[:, bk, u_ * 128:(u_ + 1) * 128], ckvn[:, tti * 128:(tti + 1) * 128], wukv[:, 768 + h * 128:768 + (h + 1) * 128])
                                    b.copy(vt[:, hb, t4_ * 4:(t4_ + 1) * 4, :], ps[:, bk, :].rearrange("p (a c) -> p a c", c=128), eng='dve')
                                U.append(uv)
                            pend = []
                            for wh in range(2):
                                for bi in range(NBLK):
                                    def usq(wh=wh, bi=bi):
                                        k_ = (wh * 4 + bi) % 2
                                        q1, q2 = sqm[:, k_, 0, :], sqm[:, k_, 1, :]
                                        if wh == 0:
                                            b.tt(q1, qnT[:, hb, blk(bi)], qnT[:, hb, blk(bi)], ALU.mult, eng='pool')
                                            b.tt(q2[hp, :], qrT[hp, h // 2, blk(bi)], qrT[hp, h // 2, blk(bi)], ALU.mult, eng='pool')
                                        else:
                                            b.tt(q1, knT[:, hb, blk(bi)], knT[:, hb, blk(bi)], ALU.mult, eng='pool')
                                            b.tt(q2[hp, :], krope[hp, h % 2, blk(bi)], krope[hp, h % 2, blk(bi)], ALU.mult, eng='pool')

                                    def umm(wh=wh, bi=bi):
                                        k_ = (wh * 4 + bi) % 2
                                        q1, q2 = sqm[:, k_, 0, :], sqm[:, k_, 1, :]
                                        bk = nb()
                                        b.mm(ps[:, bk, :], ones[:], q1, start=True, stop=False)
                                        b.mm(ps[:, bk, :], ones[hp, :], q2[hp, :], start=False, stop=True)
                                        b.rmax(mx[:, hb, wh * 4 + bi:wh * 4 + bi + 1], ps[:, bk, :])
                                    pend.append((usq, umm))
                            prev = None
                            for (usq, umm) in pend:
                                def unit(usq=usq, prev=prev):
                                    usq()
                                    if prev is not None:
                                        prev()
                                U.append(unit)
                                prev = umm

                            def ufin(prev=prev):
                                prev()
                                b.rmax(mx[:, hb, 8:9], mx[:, hb, 0:4])
                                b.rmax(mx[:, hb, 9:10], mx[:, hb, 4:8])
                                b.tt(mx[:, hb, 10:11], mx[:, hb, 8:9], mx[:, hb, 9:10], ALU.mult)
                                b.act(mx[:, hb, 11:12], mx[:, hb, 10:11], AF.Sqrt)
                                b.ts(negM[:, h:h + 1], mx[:, hb, 11:12], -SC * 1.02, ALU.mult)
                            U.append(ufin)
                            return U
                        units_of[h] = make_units()

                        def prep(h=h):
                            for u in units_of[h]:
                                u()
                            units_of[h] = []

                        for qb in range(NBLK):
                            def mk(h=h, hb=hb, hp=hp, qb=qb, idx=len(iters), prep=prep):
                                nkt = 4 * qb + 4
                                bO = 2 + 2 * (idx % 2)
                                bSm = 3 + 2 * (idx % 2)
                                if idx % 2 == 0:
                                    bO, bSm = 2, 3
                                else:
                                    bO, bSm = 0, 1

                                trim = qb > 0
                                korder = [4 * qb + d for d in range(4)] + list(range(4 * qb))

                                def cols(kt):
                                    d_ = kt - 4 * qb
                                    return 128 * d_ if (trim and d_ > 0) else 0

                                def qk(i, bank):
                                    kt = korder[i]
                                    c0 = cols(kt)
                                    ks = slice(kt * 128, (kt + 1) * 128)
                                    qs_ = slice(qb * 512 + c0, (qb + 1) * 512)
                                    b.mm(ps[:, bank, c0:512], knT[:, hb, ks], qnT[:, hb, qs_], start=True, stop=False)
                                    b.mm(ps[:, bank, c0:512], krope[:, h % 2, ks], qrT[:, h // 2, qs_], start=False, stop=True)

                                def ex(i, bank, pt):
                                    kt = korder[i]
                                    d_ = kt - 4 * qb
                                    c0 = cols(kt)
                                    if d_ <= 0 or trim:
                                        b.act(pt[:, c0:], ps[:, bank, c0:], AF.Exp, bias=negM[:, h:h + 1], scale=SC)
                                    else:
                                        b.memset(pt[:, 0:128 * d_], 0.0)
                                        b.act(pt[:, 128 * d_:], ps[:, bank, 128 * d_:], AF.Exp, bias=negM[:, h:h + 1], scale=SC)
                                    if d_ >= 0:
                                        b.memset(pt[64:128, 128 * d_:128 * d_ + 64], 0.0)

                                def pv(i, pt):
                                    kt = korder[i]
                                    c0 = cols(kt)
                                    b.mm(ps[:, bO, c0:512], vt[:, hb, kt, :], pt[:, c0:], start=(i == 0), stop=(i == nkt - 1))
                                    b.mm(ps[:, bSm, c0:512], ones[:], pt[:, c0:], start=(i == 0), stop=(i == nkt - 1))

                                def post():
                                    e = idx % 2
                                    b.copy(evO[:, e, :], ps[:, bO, :], eng='dve')
                                    b.copy(evS[:, e, :], ps[:, bSm, :], eng='dve')
                                    b.recip(evS[:, e, :], evS[:, e, :])
                                    b.tt(ydT[:, h, blk(qb)], evO[:, e, :], evS[:, e, :], ALU.mult)
                                return dict(n=nkt, qk=qk, ex=ex, pv=pv, post=post, pre=(prep if qb == 0 else None), bg=(lambda: units_of.get(h + 1)))
                            iters.append(mk())
                    attn_pipeline(iters, [4, 5, 6], PT, early_pre=False)
                P.barrier()
                if dbg_stage == 'yd':
                    dumpN(ydT, 6)
                    return
                with ExitStack() as s4:
                    wo = b.sb("wo1", [128, 8, D], BF16, s4)
                    osb_[0] = b.sb("osb1", [128, 8, 512], F32, s4)
                    b.dma(wo[:, 0:4], wview(Ld['w_out'], 0, D)[:, 0:4], q='pool')
                    b.dma(wo[:, 4:8], wview(Ld['w_out'], 0, D)[:, 4:8], q='pool')
                    for bi in range(NBLK):
                        proj_block(wo, 8, lambda kt: (ycT[:, kt, blk(bi)] if kt < 2 else ydT[:, kt - 2, blk(bi)]), 1, 8, bi)
            P.barrier()

        if dbg_stage in ('yc', 'yd', 'l1only', 'h3l1'):
            s5_params()
            layer1_mixer()
            if dbg_stage in ('l1only', 'h3l1'):
                b.dma(outT.rearrange("(kt p) s -> p kt s", p=128), hT[:])
        else:
            s5_params()
            layer0_mixer()
            if dbg_stage not in ('ya', 'yb'):
                if dbg_stage != 'h1':
                    ffn(0)
                if dbg_stage is None:
                    layer1_mixer()
                    ffn(1)
                b.dma(outT.rearrange("(kt p) s -> p kt s", p=128), hT[:])
        P.finalize(sems)
        nc._prog_stats = P.stats
    return nc


def _cols(v, n):
    return np.ascontiguousarray(np.asarray(v, np.float32).reshape(n, 128).T)


def prep_inputs(inp):
    f = np.float32
    shared = {}
    shared['ident'] = np.eye(128, dtype=f)
    for l in range(2):
        vec = np.concatenate([_cols(inp[f'l{l}_mix_pre'], 8), _cols(inp[f'l{l}_mix_post'], 8),
                              _cols(inp[f'l{l}_ffn_pre'], 8), _cols(inp[f'l{l}_ffn_post'], 8)], axis=1)
        shared[f'l{l}_vec'] = np.ascontiguousarray(vec, f)
        shared[f'l{l}_w_out'] = np.ascontiguousarray(inp[f'l{l}_w_out'], f)
        wg_ = np.asarray(inp[f'l{l}_w_gate'], f).reshape(8, 128, NFT, 128)
        wu_ = np.asarray(inp[f'l{l}_w_up'], f).reshape(8, 128, NFT, 128)
        gu = np.concatenate([wg_.transpose(2, 1, 0, 3), wu_.transpose(2, 1, 0, 3)], axis=3)
        shared[f'l{l}_wgu'] = np.ascontiguousarray(gu.reshape(NFT, 128, 8 * 256))
        wd_ = np.asarray(inp[f'l{l}_w_down'], f).reshape(NFT, 128, 8, 128)
        shared[f'l{l}_wd'] = np.ascontiguousarray(wd_.transpose(2, 1, 0, 3).reshape(8, 128, NFT * 128))
    w = np.asarray(inp['l0_w_in'], f)
    perm64 = np.arange(64)
    perm64[0:8] = np.arange(8, 16)
    perm64[8:16] = np.arange(0, 8)
    cols = []
    for base in (0, 512):
        for i in range(4):
            c0 = base + 128 * i
            cols.append(np.arange(c0, c0 + 128))
            cols.append(np.concatenate([c0 + perm64, c0 + 64 + perm64]))
    for i in range(4):
        cols.append(np.arange(1536 + 128 * i, 1536 + 128 * (i + 1)))
        cols.append(np.arange(2048 + 128 * i, 2048 + 128 * (i + 1)))
    cols.append(np.arange(1024, 1536))
    cols = np.concatenate(cols)
    shared['l0_w_inx'] = np.ascontiguousarray(w[:, cols])
    rot = 16
    inv = 500000.0 ** (-np.arange(0, rot, 2, dtype=np.float64) / rot)
    ang = np.arange(S, dtype=np.float64)[:, None] * inv[None, :]
    cs, sn = np.cos(ang).astype(f), np.sin(ang).astype(f)
    C = np.ones((128, S), f)
    Sg = np.zeros((128, S), f)
    for p in range(128):
        d = p % 64
        if d < 8:
            C[p] = cs[:, d]
            Sg[p] = -sn[:, d]
        elif d < 16:
            C[p] = cs[:, d - 8]
            Sg[p] = sn[:, d - 8]
    shared['l0_rope'] = np.ascontiguousarray(np.stack([C, Sg], axis=1))
    lamv = np.concatenate([inp['l0_lambda_q1'], inp['l0_lambda_k1'], inp['l0_lambda_q2'], inp['l0_lambda_k2']]).astype(f)
    shared['l0_lam'] = np.ascontiguousarray(np.tile(lamv[None, :], (128, 1)))
    cvv = np.zeros((128, 16), f)
    cvv[:, 0] = inp['l0_subln']
    cvv[:, 1:5] = _cols(inp['l0_dw_b'], 4)
    cvv[:, 5:9] = _cols(inp['l0_conv_ln_g'], 4)
    cvv[:, 9:13] = _cols(inp['l0_conv_ln_b'], 4)
    shared['l0_cv'] = cvv
    dw = np.asarray(inp['l0_dw_w'], f)
    dwT = dw.T.reshape(4, 128, 31).transpose(1, 0, 2).reshape(128, 124)
    shared['l0_dwT'] = np.ascontiguousarray(dwT)
    w = np.asarray(inp['l1_w_in'], f)
    p64 = np.concatenate([np.arange(32, 64), np.arange(0, 32)])
    kr = w[:, 640:704]
    shared['l1_w_inx'] = np.ascontiguousarray(np.concatenate([w[:, 0:640], kr, kr, kr[:, p64], kr[:, p64]], axis=1))
    wq = np.asarray(inp['l1_w_uq'], f).reshape(256, 6, 192)
    nope = wq[:, :, 0:128].reshape(256, 768)
    rp = wq[:, :, 128:192]
    shared['l1_w_uqx'] = np.ascontiguousarray(np.concatenate([nope, rp.reshape(256, 384), rp[:, :, p64].reshape(256, 384)], axis=1))
    wkv = np.asarray(inp['l1_w_ukv'], f).reshape(128, 6, 256)
    shared['l1_w_ukvx'] = np.ascontiguousarray(np.concatenate([wkv[:, :, 0:128].reshape(128, 768), wkv[:, :, 128:256].reshape(128, 768)], axis=1))
    shared['l1_w_glu'] = np.ascontiguousarray(inp['l1_w_glu'], f)
    inv = 10000.0 ** (-np.arange(0, 64, 2, dtype=np.float64) / 64)
    ang = np.arange(S, dtype=np.float64)[:, None] * inv[None, :]
    cs, sn = np.cos(ang).astype(f), np.sin(ang).astype(f)
    C = np.zeros((128, S), f)
    Sg = np.zeros((128, S), f)
    for p in range(128):
        d = p % 64
        C[p] = cs[:, d % 32]
        Sg[p] = -sn[:, d] if d < 32 else sn[:, d - 32]
    shared['l1_rope'] = np.ascontiguousarray(np.stack([C, Sg], axis=1))
    cv1 = np.zeros((128, 16), f)
    cv1[:, 0:2] = _cols(inp['l1_q_norm'], 2)
    cv1[:, 2] = inp['l1_kv_norm']
    cv1[:, 3:5] = _cols(np.asarray(inp['l1_d_skip'], f).reshape(-1), 2)
    cv1[:, 5:7] = _cols(inp['l1_b_glu'], 2)
    shared['l1_cv'] = cv1
    ssm = np.concatenate([_cols(np.asarray(inp['l1_a_re'], f).reshape(-1), 8),
                          _cols(np.asarray(inp['l1_a_im'], f).reshape(-1), 8),
                          _cols(np.repeat(np.asarray(inp['l1_log_dt'], f), 64), 8)], axis=1)
    shared['l1_ssm'] = np.ascontiguousarray(ssm, f)
    Bp = np.zeros((2, 8, 128, 128), f)
    Cp = np.zeros((2, 8, 128, 128), f)
    for a, (bsrc, csrc) in enumerate(((inp['l1_b_re'], inp['l1_c_re']), (inp['l1_b_im'], inp['l1_c_im']))):
        bsrc = np.asarray(bsrc, f)
        csrc = np.asarray(csrc, f)
        for g in range(16):
            j, e = g // 2, g % 2
            r0 = 32 * (j % 4) + 16 * e
            Bp[a, j, r0:r0 + 16, 64 * e:64 * e + 64] = bsrc[g].T
            Cp[a, j, 64 * e:64 * e + 64, r0:r0 + 16] = csrc[g].T
    shared['l1_Bpad'] = np.ascontiguousarray(Bp.reshape(16, 128, 128))
    shared['l1_Cpad'] = np.ascontiguousarray(Cp.reshape(16, 128, 128))
    shared['l1_BpadT'] = np.ascontiguousarray(Bp.transpose(0, 1, 3, 2).reshape(16, 128, 128))
    return shared


_NC_CACHE = {}


def kernel(**inputs):
    dbg = os.environ.get('KDBG') or None
    inp = {k: np.asarray(v) for k, v in inputs.items()}
    shared = prep_inputs(inp)
    x = np.asarray(inp['x'], np.float32)
    ncores = 8
    if dbg not in _NC_CACHE:
        _NC_CACHE[dbg] = build(dbg)
    nc = _NC_CACHE[dbg]
    in_maps = []
    for c in range(ncores):
        m = dict(shared)
        m['xT'] = np.ascontiguousarray(x[c].T)
        in_maps.append(m)
    res = run_bass_kernel_spmd(nc, in_maps, core_ids=list(range(ncores)))
    out = np.stack([np.ascontiguousarray(r['outT'].T) for r in res.results], axis=0)
    return out.astype(np.float32)
```

```python
import numpy as np
import concourse.bass as bass
import concourse.mybir as mybir

SEM_CAP = 30000
STRICT = True


SB_ADDR = {}
SB_ALIAS = {}
SB_RANGES = []
NO_BARRIER = [True]


def sb_register(name, start, end, esz):
    SB_ADDR[name] = (start, esz)
    SB_ALIAS[name] = [n for (n, s0, e0) in SB_RANGES if s0 < end and start < e0 and n != name]
    SB_RANGES[:] = [r for r in SB_RANGES if r[0] != name]
    SB_RANGES.append((name, start, end))


def _box(ap):
    t = ap.tensor
    name = t.name
    dims = [list(d) for d in ap.ap]
    off = int(ap.offset)
    space = str(ap.space)
    if 'DRAM' in space.upper() or 'HBM' in space.upper():
        span = sum((c - 1) * abs(s) for s, c in dims) + 1
        return (name, 0, 1, off, off + span)
    shp = list(t.shape)
    per = 1
    for s in shp[1:]:
        per *= s
    pstep, pcnt = dims[0]
    p0 = off // per
    f0 = off % per
    if pstep == 0:
        pcnt_eff = 1
    else:
        pcnt_eff = pcnt
    span = sum((c - 1) * abs(s) for s, c in dims[1:]) + 1
    f1 = f0 + span
    if name in SB_ADDR:
        st_, esz = SB_ADDR[name]
        return (name, p0, p0 + pcnt_eff, st_ + f0 * esz, st_ + f1 * esz)
    if 'PSUM' in space.upper():
        eb = 2048 // mybir.dt.size(ap.dtype) if hasattr(mybir.dt, 'size') else 512
        f0 = (f0 // eb) * eb
        f1 = ((f1 + eb - 1) // eb) * eb
        return (name, 0, 128, f0, f1)
    return (name, p0, p0 + pcnt_eff, f0, f1)


def _ov(a, b):
    return a[1] < b[2] and b[1] < a[2] and a[3] < b[4] and b[3] < a[4]


def _cov(a, b):
    return a[1] <= b[1] and a[2] >= b[2] and a[3] <= b[3] and a[4] >= b[4]


class Prog:
    ENG = ('pe', 'act', 'dve', 'pool', 'sp')

    def __init__(self, nc):
        self.nc = nc
        self.eng = {'pe': nc.tensor, 'act': nc.scalar, 'dve': nc.vector,
                    'pool': nc.gpsimd, 'sp': nc.sync}
        self.ins = []

    def op(self, e, fn, r, w):
        self.ins.append(dict(e=e, fn=fn, r=[_box(a) for a in r], w=[_box(a) for a in w], dma=False))

    def barrier(self):
        if NO_BARRIER[0]:
            return
        self.ins.append(dict(e=None, fn=None, r=[], w=[], dma=False, bar=True))

    def dma(self, out, in_, q='sp'):
        def fn(eng, out=out, in_=in_):
            return eng.dma_start(out=out, in_=in_)
        self.ins.append(dict(e=q, fn=fn, r=[_box(in_)], w=[_box(out)], dma=True))

    def finalize(self, sems_pool):
        nc = self.nc
        ins = self.ins
        n = len(ins)
        hist = {}
        seq = {}
        cnt = {e: 0 for e in self.ENG}
        observed = {e: {p: -1 for p in self.ENG} for e in self.ENG}
        obs_dma = {e: set() for e in self.ENG}
        snap = {}
        waits = [[] for _ in range(n)]
        signal = [False] * n
        eng_list = {e: [] for e in self.ENG}
        for i, I in enumerate(ins):
            if I.get('bar'):
                bw = []
                for E in self.ENG:
                    for Pn in self.ENG:
                        if cnt[Pn] == 0 or (Pn == E and E in ('sp', 'pe')):
                            continue
                        sp_ = cnt[Pn] - 1
                        if observed[E][Pn] >= sp_:
                            continue
                        bw.append((E, Pn, sp_))
                        signal[eng_list[Pn][sp_]] = True
                        observed[E][Pn] = sp_
                    obs_dma[E] = set(j for j in range(i) if ins[j]['dma'])
                waits[i] = bw
                hist = {}
                continue
            E = I['e']
            s = cnt[E]
            cnt[E] += 1
            seq[i] = (E, s)
            eng_list[E].append(i)
            deps = set()
            rawset = set()
            for b in I['r']:
                for nm in [b[0]] + SB_ALIAS.get(b[0], []):
                    for (hb, j, isw) in hist.get(nm, ()):
                        if isw and _ov(hb, b):
                            deps.add(j)
                            rawset.add(j)
            for b in I['w']:
                for nm in [b[0]] + SB_ALIAS.get(b[0], []):
                    for (hb, j, isw) in hist.get(nm, ()):
                        if _ov(hb, b):
                            deps.add(j)
            need = {}
            for j in deps:
                J = ins[j]
                if J['dma']:
                    if j not in obs_dma[E]:
                        waits[i].append(('d', j))
                        obs_dma[E].add(j)
                        signal[j] = True
                        sj = snap[j]
                        for p in self.ENG:
                            if sj[p] > observed[E][p]:
                                observed[E][p] = sj[p]
                    continue
                P, sp_ = seq[j]
                if P == E and not I['dma']:
                    if E == 'pe':
                        continue
                    if (not STRICT) and j not in rawset:
                        continue
                if sp_ > need.get(P, -1):
                    need[P] = sp_
            for P, sp_ in need.items():
                if observed[E][P] >= sp_:
                    continue
                j = eng_list[P][sp_]
                waits[i].append(('e', P, sp_))
                signal[j] = True
                observed[E][P] = sp_
                sj = snap[j]
                for p in self.ENG:
                    if sj[p] > observed[E][p]:
                        observed[E][p] = sj[p]
            snap[i] = dict(observed[E])
            for b in I['w']:
                lst = hist.setdefault(b[0], [])
                lst[:] = [h for h in lst if not _cov(b, h[0])]
                lst.append((b, i, True))
            for b in I['r']:
                lst = hist.setdefault(b[0], [])
                if not I['dma']:
                    lst[:] = [h for h in lst if h[2] or ins[h[1]]['dma'] or ins[h[1]]['e'] != E or not _cov(b, h[0])]
                lst.append((b, i, False))
        sem_iter = iter(sems_pool)
        eng_sem = {}
        eng_sig = {e: 0 for e in self.ENG}
        sem_of = {}
        for e in self.ENG:
            eng_sem[e] = next(sem_iter)
        dma_sems = [next(sem_iter) for _ in range(64)]
        dma_cnt = [0] * len(dma_sems)
        dma_last = [None] * len(dma_sems)
        dma_rr = {'sp': 0, 'pool': 0}
        bar_seen = {e: [0] * len(dma_sems) for e in self.ENG}
        dma_rng = {'sp': (0, 32), 'pool': (32, 64)}
        final_dma = []
        nw = 0
        for i, I in enumerate(ins):
            if I.get('bar'):
                for (E, Pn, sp_) in waits[i]:
                    sm, val = sem_of[eng_list[Pn][sp_]]
                    self.eng[E].wait_ge(sm, val)
                    nw += 1
                for E in self.ENG:
                    for k, sm in enumerate(dma_sems):
                        if dma_cnt[k] > bar_seen[E][k]:
                            self.eng[E].wait_ge(sm, dma_cnt[k])
                            bar_seen[E][k] = dma_cnt[k]
                continue
            E = I['e']
            eng = self.eng[E]
            for wt in waits[i]:
                if wt[0] == 'd':
                    sm, val = sem_of[wt[1]]
                else:
                    sm, val = sem_of[eng_list[wt[1]][wt[2]]]
                eng.wait_ge(sm, val)
                nw += 1
            if I['dma']:
                lo_, hi_ = dma_rng[E]
                k = lo_ + dma_rr[E]
                dma_rr[E] = (dma_rr[E] + 1) % (hi_ - lo_)
                if dma_last[k] is not None:
                    pj = dma_last[k]
                    if pj not in obs_dma[E] or True:
                        eng.wait_ge(dma_sems[k], dma_cnt[k])
                dma_cnt[k] += 16
                inst = I['fn'](eng)
                inst.then_inc(dma_sems[k], 16)
                sem_of[i] = (dma_sems[k], dma_cnt[k])
                dma_last[k] = i
                continue
            inst = I['fn'](eng)
            if signal[i]:
                if eng_sig[E] >= SEM_CAP:
                    eng_sem[E] = next(sem_iter)
                    eng_sig[E] = 0
                eng_sig[E] += 1
                inst.then_inc(eng_sem[E], 1)
                sem_of[i] = (eng_sem[E], eng_sig[E])
        for k, sm in enumerate(dma_sems):
            if dma_cnt[k] > 0:
                self.eng['sp'].wait_ge(sm, dma_cnt[k])
        self.stats = dict(n=n, waits=nw, per_eng=dict(cnt))
        self.dbg_waits = waits
        self.dbg_seq = seq
        self.dbg_englist = eng_list

import os
import math
from contextlib import ExitStack
from concourse.bass_utils import run_bass_kernel_spmd

F32 = mybir.dt.float32
BF16 = mybir.dt.bfloat16
AF = mybir.ActivationFunctionType
ALU = mybir.AluOpType
AX = mybir.AxisListType

S = 2048
D = 1024
NB = 512
NBLK = 4
DFF = 2816
NFT = 22
RMS_EPS = 1e-6
LN_EPS = 1e-5


def _isnum(x):
    return isinstance(x, (int, float))


class B:
    def __init__(self, nc, P, es):
        self.nc = nc
        self.P = P
        self.es = es
        self.ps = es.enter_context(nc.psum_tensor("ps", [128, 8, 512], F32))
        self._bank = 0
        self._consts = {}
        self.cbuf = self.sb("cbuf", [128, 16], F32)
        self._nc = 0

    def sb(self, name, shape, dt, es=None):
        t = (es or self.es).enter_context(self.nc.sbuf_tensor(name, shape, dt))
        per = 1
        for d_ in shape[1:]:
            per *= d_
        esz = mybir.dt.size(dt)
        end = int(self.nc.sbuf_base)
        sb_register(name, end - per * esz, end, esz)
        return t

    def bank(self):
        k = self._bank
        self._bank = (k + 1) % 8
        return k

    def const(self, val):
        if val not in self._consts:
            ap = self.cbuf[:, self._nc:self._nc + 1]
            self._nc += 1
            self.memset(ap, float(val), eng='dve')
            self._consts[val] = ap
        return self._consts[val]

    def mm(self, out, lhsT, rhs, start=True, stop=True):
        self.P.op('pe', lambda e: e.matmul(out, lhsT, rhs, start=start, stop=stop), [lhsT, rhs], [out])

    def act(self, out, in_, func, bias=None, scale=1.0):
        r = [in_]
        if func == AF.Copy:
            bias = 0.0
        else:
            if bias is None:
                bias = self.const(0.0)
            elif _isnum(bias):
                bias = self.const(bias)
            npart = out.shape[0]
            if bias.shape[0] != npart:
                per = 1
                for d_ in list(out.tensor.shape)[1:]:
                    per *= d_
                p0 = int(out.offset) // per
                bias = bias[p0:p0 + npart, :]
            r.append(bias)
        if not _isnum(scale):
            r.append(scale)
        self.P.op('act', lambda e: e.activation(out=out, in_=in_, func=func, bias=bias, scale=scale), r, [out])

    def tt(self, out, a, b, op, eng='dve'):
        self.P.op(eng, lambda e: e.tensor_tensor(out=out, in0=a, in1=b, op=op), [a, b], [out])

    def ts(self, out, a, s1, op0, s2=None, op1=None, eng='dve'):
        r = [a]
        if not _isnum(s1):
            r.append(s1)
        if s2 is not None and not _isnum(s2):
            r.append(s2)
        if op1 is None:
            self.P.op(eng, lambda e: e.tensor_scalar(out=out, in0=a, scalar1=s1, scalar2=None, op0=op0), r, [out])
        else:
            self.P.op(eng, lambda e: e.tensor_scalar(out=out, in0=a, scalar1=s1, scalar2=s2, op0=op0, op1=op1), r, [out])

    def stt(self, out, in0, scalar, in1, op0, op1):
        r = [in0, in1]
        if not _isnum(scalar):
            r.append(scalar)
        self.P.op('dve', lambda e: e.scalar_tensor_tensor(out=out, in0=in0, scalar=scalar, in1=in1, op0=op0, op1=op1), r, [out])

    def copy(self, out, in_, eng='dve'):
        if eng == 'act':
            self.act(out, in_, AF.Copy)
        else:
            self.P.op(eng, lambda e: e.tensor_copy(out=out, in_=in_), [in_], [out])

    def memset(self, ap, val, eng='pool'):
        self.P.op(eng, lambda e: e.memset(ap, val), [], [ap])

    def recip(self, out, in_):
        self.P.op('dve', lambda e: e.reciprocal(out=out, in_=in_), [in_], [out])

    def rmax(self, out, in_):
        self.P.op('dve', lambda e: e.reduce_max(out=out, in_=in_, axis=AX.X), [in_], [out])

    def rsum(self, out, in_):
        self.P.op('dve', lambda e: e.reduce_sum(out=out, in_=in_, axis=AX.X), [in_], [out])

    def scan(self, out, d0, d1, init):
        r = [d0, d1]
        if not _isnum(init):
            r.append(init)
        self.P.op('dve', lambda e: e.tensor_tensor_scan(out=out, data0=d0, data1=d1, initial=init, op0=ALU.mult, op1=ALU.add), r, [out])

    def dma(self, out, in_, q='sp'):
        self.P.dma(out, in_, q=q)

    def rstd(self, out, in_, scale, eps, tmp):
        self.act(tmp, in_, AF.Ln, bias=eps, scale=scale)
        self.act(out, tmp, AF.Exp, scale=-0.5)


def wview(w, c0, ncols):
    return w[:, c0:c0 + ncols].rearrange("(kt p) n -> p kt n", p=128)


def build(dbg_stage=None):
    nc = bass.Bass("TRN2", target_bir_lowering=False)
    SB_ADDR.clear()
    SB_ALIAS.clear()
    SB_RANGES[:] = []

    def din(name, shape):
        return nc.dram_tensor(name, list(shape), F32, kind="ExternalInput").ap()

    xT = din("xT", [D, S])
    ident_d = din("ident", [128, 128])
    L = []
    for l in range(2):
        d = {}
        d['vec'] = din(f"l{l}_vec", [128, 32])
        d['w_out'] = din(f"l{l}_w_out", [D, D])
        d['wgu'] = din(f"l{l}_wgu", [NFT, 128, 8 * 256])
        d['wd'] = din(f"l{l}_wd", [8, 128, NFT * 128])
        L.append(d)
    L[0]['w_in'] = din("l0_w_inx", [D, 3584])
    L[0]['rope'] = din("l0_rope", [128, 2, S])
    L[0]['lam'] = din("l0_lam", [128, 256])
    L[0]['cv'] = din("l0_cv", [128, 16])
    L[0]['dwT'] = din("l0_dwT", [128, 4 * 31])
    L[1]['w_in'] = din("l1_w_inx", [D, 896])
    L[1]['w_uq'] = din("l1_w_uqx", [256, 1536])
    L[1]['w_ukv'] = din("l1_w_ukvx", [128, 1536])
    L[1]['w_glu'] = din("l1_w_glu", [256, 256])
    L[1]['rope'] = din("l1_rope", [128, 2, S])
    L[1]['cv'] = din("l1_cv", [128, 16])
    L[1]['ssm'] = din("l1_ssm", [128, 24])
    L[1]['Bpad'] = din("l1_Bpad", [16, 128, 128])
    L[1]['Cpad'] = din("l1_Cpad", [16, 128, 128])
    L[1]['BpadT'] = din("l1_BpadT", [16, 128, 128])
    outT = nc.dram_tensor("outT", [D, S], F32, kind="ExternalOutput").ap()

    with ExitStack() as es:
        sems = [es.enter_context(nc.semaphore(f"s{i}")) for i in range(80)]
        P = Prog(nc)
        b = B(nc, P, es)
        ps = b.ps
        hT = b.sb("hT", [128, 8, S], F32)
        ones = b.sb("ones", [128, 128], BF16)
        ident = b.sb("ident_sb", [128, 128], F32)
        sq = b.sb("sq", [128, 4, 512], BF16)
        tmpf = b.sb("tmpf", [128, 4, 512], F32)
        osb_ = [None]
        rstd_t = b.sb("rstd_t", [128, 2, 512], F32)
        vec = [b.sb(f"vec{l}", [128, 32], F32) for l in range(2)]
        sqi = [0]
        tfi = [0]

        def sq_next():
            k = sqi[0]
            sqi[0] = (k + 1) % 4
            return sq[:, k, :]

        def tf_next():
            k = tfi[0]
            tfi[0] = (k + 1) % 4
            return tmpf[:, k, :]

        b.memset(ones[:], 1.0)
        b.dma(ident[:], ident_d)
        for l in range(2):
            b.dma(vec[l][:], L[l]['vec'])
        b.dma(hT[:], xT.rearrange("(kt p) s -> p kt s", p=128))

        def blk(i):
            return slice(i * NB, (i + 1) * NB)

        def sumsq_mm(bankS, srcs):
            n = len(srcs)
            for i, sap in enumerate(srcs):
                q = sq_next()
                kp = sap.shape[0]
                b.act(q[0:kp, :], sap, AF.Square)
                b.mm(ps[:, bankS, :], ones[0:kp, :], q[0:kp, :], start=(i == 0), stop=(i == n - 1))

        def prenorm_block(l, goff, bi, tT_dst):
            bankS = b.bank()
            sumsq_mm(bankS, [hT[:, kt, blk(bi)] for kt in range(8)])
            r = rstd_t[:, bi % 2, :]
            b.rstd(r, ps[:, bankS, :], 1.0 / D, RMS_EPS, tf_next())
            for kt in range(8):
                b.stt(tT_dst[:, kt, :], hT[:, kt, blk(bi)], vec[l][:, goff + kt:goff + kt + 1], r, ALU.mult, ALU.mult)

        def postnorm_residual(l, goff, bi, bankS):
            r = rstd_t[:, bi % 2, :]
            b.rstd(r, ps[:, bankS, :], 1.0 / D, RMS_EPS, tf_next())
            for mt in range(8):
                t = tf_next()
                b.stt(t, osb_[0][:, mt, :], vec[l][:, goff + mt:goff + mt + 1], r, ALU.mult, ALU.mult)
                b.tt(hT[:, mt, blk(bi)], hT[:, mt, blk(bi)], t, ALU.add, eng='dve')

        def proj_block(wsb, nkt, rhs_fn, l, goff, bi, lhs_fn=None):
            bankS = b.bank()
            pend = None
            for mt in range(8):
                bk = b.bank()
                if bk == bankS:
                    bk = b.bank()
                for kt in range(nkt):
                    lhsT = lhs_fn(kt, mt) if lhs_fn else wsb[:, kt, mt * 128:(mt + 1) * 128]
                    b.mm(ps[:, bk, :], lhsT, rhs_fn(kt), start=(kt == 0), stop=(kt == nkt - 1))
                if pend is not None:
                    b.mm(ps[:, bankS, :], ones[:], pend[0], start=(pend[1] == 0), stop=False)
                b.act(osb_[0][:, mt, :], ps[:, bk, :], AF.Copy)
                q = sq_next()
                b.act(q, ps[:, bk, :], AF.Square)
                pend = (q, mt)
            b.mm(ps[:, bankS, :], ones[:], pend[0], start=False, stop=True)
            postnorm_residual(l, goff, bi, bankS)

        def ffn(l):
            with ExitStack() as fs:
                tTb = b.sb(f"ffn_tT{l}", [128, 2, 8, 1024], BF16, fs)
                aT = b.sb(f"ffn_aT{l}", [128, NFT, 1024], BF16, fs)
                wgu = b.sb(f"ffn_wgu{l}", [128, 3, 8, 256], BF16, fs)
                wd = b.sb(f"ffn_wd{l}", [128, 3, NFT, 128], BF16, fs)
                sg = b.sb(f"ffn_sg{l}", [128, 2, 512], BF16, fs)
                osb_[0] = b.sb(f"ffn_osb{l}", [128, 8, 512], F32, fs)
                wi = 0
                di = 0
                for bl in range(2):
                    prenorm_block(l, 16, bl, tTb[:, 0, :, bl * 512:(bl + 1) * 512])
                for half in range(2):
                    tT = tTb[:, half]
                    for ft in range(NFT):
                        if half == 0 and ft in (8, 14):
                            bl_ = 0 if ft == 8 else 1
                            prenorm_block(l, 16, 2 + bl_, tTb[:, 1, :, bl_ * 512:(bl_ + 1) * 512])
                        wb = wgu[:, wi % 3]
                        wi += 1
                        b.dma(wb, L[l]['wgu'][ft].rearrange("p (kt c) -> p kt c", c=256), q='pool')
                        for bl in range(2):
                            tk = slice(bl * 512, (bl + 1) * 512)
                            bg = b.bank()
                            bu = b.bank()
                            for kt in range(8):
                                b.mm(ps[:, bg, :], wb[:, kt, 0:128], tT[:, kt, tk], start=(kt == 0), stop=(kt == 7))
                            for kt in range(8):
                                b.mm(ps[:, bu, :], wb[:, kt, 128:256], tT[:, kt, tk], start=(kt == 0), stop=(kt == 7))
                            s_ = sg[:, bl, :]
                            b.act(s_, ps[:, bg, :], AF.Silu)
                            b.tt(aT[:, ft, tk], ps[:, bu, :], s_, ALU.mult)
                    for bl in range(2):
                        bi = half * 2 + bl
                        tk = slice(bl * 512, (bl + 1) * 512)
                        bankS = b.bank()
                        pend = None
                        for mt in range(8):
                            wdb = wd[:, di % 3]
                            di += 1
                            b.dma(wdb, L[l]['wd'][mt].rearrange("p (kt c) -> p kt c", c=128), q='pool')
                            bk = b.bank()
                            if bk == bankS:
                                bk = b.bank()
                            for kt in range(NFT):
                                b.mm(ps[:, bk, :], wdb[:, kt, :], aT[:, kt, tk], start=(kt == 0), stop=(kt == NFT - 1))
                            if pend is not None:
                                b.mm(ps[:, bankS, :], ones[:], pend[0], start=(pend[1] == 0), stop=False)
                            b.act(osb_[0][:, mt, :], ps[:, bk, :], AF.Copy)
                            q = sq_next()
                            b.act(q, ps[:, bk, :], AF.Square)
                            pend = (q, mt)
                        b.mm(ps[:, bankS, :], ones[:], pend[0], start=False, stop=True)
                        postnorm_residual(l, 24, bi, bankS)
            P.barrier()

        S5P = {}

        def s5_params():
            TWO_PI = 2.0 * math.pi
            prm = b.sb("prm", [128, 24], F32)
            sc = b.sb("ssc", [128, 48, 8], F32)
            ki = b.sb("ski", [128, 8], mybir.dt.int32)
            b.dma(prm[:], L[1]['ssm'])
            lr, li, ldt = prm[:, 0:8], prm[:, 8:16], prm[:, 16:24]
            V = lambda i: sc[:, i, :]
            dt_, rr_, th, kf, thp, s4, c4, s2, c2, s1_, c1 = [V(i) for i in range(11)]
            b.act(dt_, ldt, AF.Exp)
            b.tt(V(11), lr, dt_, ALU.mult)
            b.act(rr_, V(11), AF.Exp)
            b.tt(th, li, dt_, ALU.mult)
            b.ts(V(11), th, 1.0 / TWO_PI, ALU.mult)
            b.copy(ki[:], V(11), eng='dve')
            b.copy(kf, ki[:], eng='dve')
            b.stt(thp, kf, -TWO_PI, th, ALU.mult, ALU.add)
            b.act(s4, thp, AF.Sin, scale=0.25)
            b.act(c4, thp, AF.Sin, bias=math.pi / 2, scale=0.25)

            def dbl(so, co, si, ci):
                b.tt(V(11), si, ci, ALU.mult)
                b.ts(so, V(11), 2.0, ALU.mult)
                b.tt(V(12), si, si, ALU.mult)
                b.ts(co, V(12), -2.0, ALU.mult, 1.0, ALU.add)
            dbl(s2, c2, s4, c4)
            dbl(s1_, c1, s2, c2)
            abre, abim, den, nre, fre, fim = [V(i) for i in range(13, 19)]
            b.tt(abre, rr_, c1, ALU.mult)
            b.tt(abim, rr_, s1_, ALU.mult)
            b.tt(V(11), lr, lr, ALU.mult)
            b.tt(V(12), li, li, ALU.mult)
            b.tt(den, V(11), V(12), ALU.add)
            b.recip(den, den)
            b.ts(nre, abre, -1.0, ALU.add)
            b.tt(V(11), nre, lr, ALU.mult)
            b.tt(V(12), abim, li, ALU.mult)
            b.tt(V(11), V(11), V(12), ALU.add)
            b.tt(fre, V(11), den, ALU.mult)
            b.tt(V(11), abim, lr, ALU.mult)
            b.tt(V(12), nre, li, ALU.mult)
            b.tt(V(11), V(11), V(12), ALU.subtract)
            b.tt(fim, V(11), den, ALU.mult)
            def cmul(ore, oim, xr, xi, yr, yi):
                b.tt(V(11), xr, yr, ALU.mult)
                b.tt(V(12), xi, yi, ALU.mult)
                b.tt(V(40), xr, yi, ALU.mult)
                b.tt(V(41), xi, yr, ALU.mult)
                b.tt(ore, V(11), V(12), ALU.subtract)
                b.tt(oim, V(40), V(41), ALU.add)
            PW = b.sb("PW", [128, 16, 8], F32)
            PWr = lambda i: PW[:, i, :]
            b.copy(PWr(0), abre, eng='dve')
            b.copy(PWr(1), abim, eng='dve')
            cmul(PWr(2), PWr(3), abre, abim, abre, abim)
            cmul(PWr(4), PWr(5), PWr(2), PWr(3), abre, abim)
            cmul(PWr(6), PWr(7), PWr(2), PWr(3), PWr(2), PWr(3))
            b.copy(PWr(8), fre, eng='dve')
            b.copy(PWr(9), fim, eng='dve')
            cmul(PWr(10), PWr(11), fre, fim, abre, abim)
            cmul(PWr(12), PWr(13), fre, fim, PWr(2), PWr(3))
            cmul(PWr(14), PWr(15), fre, fim, PWr(4), PWr(5))
            wr, wi_, wr2, wi2 = V(19), V(20), V(21), V(22)
            w4r, w4i = V(44), V(45)
            b.copy(wr, c1, eng='dve')
            b.ts(wi_, s1_, -1.0, ALU.mult)
            r4 = V(42)
            b.tt(V(43), rr_, rr_, ALU.mult)
            b.tt(r4, V(43), V(43), ALU.mult)

            def wsq(cs):
                b.tt(V(11)[:, cs], wr[:, cs], wr[:, cs], ALU.mult)
                b.tt(V(12)[:, cs], wi_[:, cs], wi_[:, cs], ALU.mult)
                b.tt(wr2[:, cs], V(11)[:, cs], V(12)[:, cs], ALU.subtract)
                b.tt(V(11)[:, cs], wr[:, cs], wi_[:, cs], ALU.mult)
                b.ts(wi2[:, cs], V(11)[:, cs], 2.0, ALU.mult)
                b.copy(wr[:, cs], wr2[:, cs], eng='dve')
                b.copy(wi_[:, cs], wi2[:, cs], eng='dve')
            wsq(slice(0, 8))
            wsq(slice(0, 8))

            PWW = b.sb("PWW", [128, 9, 2, 8], F32)
            for lv in range(9):
                b.copy(PWW[:, lv, 0, :], wr, eng='dve')
                b.copy(PWW[:, lv, 1, :], wi_, eng='dve')
                if lv < 8:
                    wsq(slice(0, 8))
            S5P.update(PW=PW, PWW=PWW, r4=r4)

        def attn_pipeline(iters, score_banks, PTbuf, LOOK=2, DEFER=6, early_pre=True):
            deferred = []
            assert len(score_banks) > LOOK
            flat = []
            for it in iters:
                for i in range(it['n']):
                    flat.append((it, i))
            N = len(flat)
            started = set()

            order = {id(it): k for k, it in enumerate(iters)}

            def start_it(it):
                if id(it) not in started:
                    started.add(id(it))
                    if it.get('pre'):
                        it['pre']()

            def emit_qk(g):
                it, i = flat[g]
                start_it(it)
                if early_pre and i == 0 and order[id(it)] + 1 < len(iters):
                    start_it(iters[order[id(it)] + 1])
                it['qk'](i, score_banks[g % len(score_banks)])
            for g in range(min(LOOK, N)):
                emit_qk(g)
            for g in range(N):
                if g + LOOK < N:
                    emit_qk(g + LOOK)
                it, i = flat[g]
                pt = PTbuf[:, g % PTbuf.shape[1], :]
                it['ex'](i, score_banks[g % len(score_banks)], pt)
                it['pv'](i, pt)
                if it.get('bg'):
                    ul = it['bg']()
                    if ul:
                        ul.pop(0)()
                for dfr in sorted([d for d in deferred if d[0] <= g], key=lambda d: d[0]):
                    dfr[1]()
                    deferred.remove(dfr)
                if i == it['n'] - 1:
                    it['post']()
                    if it.get('post_slow'):
                        for (dl, fn) in it['post_slow']:
                            deferred.append((g + dl, fn))
            for dfr in sorted(deferred, key=lambda d: d[0]):
                dfr[1]()

        def dump4(src):
            with ExitStack() as dsk:
                dbgT = b.sb("dbgT", [128, 4, S], F32, dsk)
                for ct in range(4):
                    b.copy(dbgT[:, ct, :], src[:, ct, :], eng='dve')
                b.dma(outT[0:512, :].rearrange("(kt p) s -> p kt s", p=128), dbgT[:])
                P.barrier()

        def layer0_mixer():
            Ld = L[0]
            with ExitStack() as ms:
                cv = b.sb("cv", [128, 16], F32, ms)
                b.dma(cv[:], Ld['cv'])
                yaT = b.sb("yaT", [128, 4, S], BF16, ms)
                qs = ExitStack()
                qrT = b.sb("qrT", [128, 4, S], BF16, qs)
                krT = b.sb("krT", [128, 2, 4, S], BF16, qs)
                b.memset(krT[64:128, 0], 0.0)
                b.memset(krT[0:64, 1], 0.0)
                vtm = b.sb("vtm", [128, 16, 512], BF16, qs)
                with ExitStack() as s1:
                    tTh = b.sb("tTh2", [128, 8, 1024], BF16, s1)
                    wbuf = b.sb("wbuf2", [128, 3, 8, 256], BF16, s1)
                    rope = b.sb("rope", [128, 2, 1024], F32, s1)
                    wi = 0
                    for half in range(2):
                        b.dma(rope[:], Ld['rope'][:, :, half * 1024:(half + 1) * 1024])
                        for bl in range(2):
                            prenorm_block(0, 0, half * 2 + bl, tTh[:, :, bl * 512:(bl + 1) * 512])
                        for g in range(8):
                            wb = wbuf[:, wi % 3]
                            wi += 1
                            b.dma(wb, wview(Ld['w_in'], g * 256, 256), q='pool')
                            for bl in range(2):
                                bi = half * 2 + bl
                                bA = b.bank()
                                bB = b.bank()
                                for kt in range(8):
                                    b.mm(ps[:, bA, :], wb[:, kt, 0:128], tTh[:, kt, bl * 512:(bl + 1) * 512], start=(kt == 0), stop=(kt == 7))
                                for kt in range(8):
                                    b.mm(ps[:, bB, :], wb[:, kt, 128:256], tTh[:, kt, bl * 512:(bl + 1) * 512], start=(kt == 0), stop=(kt == 7))
                                t1 = tf_next()
                                t2 = tf_next()
                                b.tt(t1, ps[:, bA, :], rope[:, 0, bl * 512:(bl + 1) * 512], ALU.mult)
                                b.tt(t2, ps[:, bB, :], rope[:, 1, bl * 512:(bl + 1) * 512], ALU.mult)
                                if g < 4:
                                    b.tt(qrT[:, g % 4, blk(bi)], t1, t2, ALU.add, eng='dve')
                                else:
                                    b.tt(krT[0:64, 0, g % 4, blk(bi)], t1[0:64, :], t2[0:64, :], ALU.add, eng='dve')
                                    b.tt(krT[64:128, 1, g % 4, blk(bi)], t1[64:128, :], t2[64:128, :], ALU.add, eng='dve')
                        for g in range(2):
                            wb = wbuf[:, wi % 3]
                            wi += 1
                            b.dma(wb, wview(Ld['w_in'], 3072 + g * 256, 256), q='pool')
                            for tt_ in range(8):
                                bk = b.bank()
                                tok = slice(tt_ * 128, (tt_ + 1) * 128)
                                for kt in range(8):
                                    b.mm(ps[:, bk, 0:256], tTh[:, kt, tok], wb[:, kt, :], start=(kt == 0), stop=(kt == 7))
                                b.copy(vtm[:, half * 8 + tt_, g * 256:(g + 1) * 256], ps[:, bk, 0:256], eng=('act' if tt_ % 2 else 'dve'))
                P.barrier()
                with ExitStack() as s3:
                    PT = b.sb("PT", [128, 4, 512], BF16, s3)
                    lam = b.sb("lam", [128, 256], F32, s3)
                    lsc = b.sb("lsc", [128, 16], F32, s3)
                    mx = b.sb("mx", [128, 16], F32, s3)
                    negM = b.sb("negM", [128, 4], F32, s3)
                    b.dma(lam[:], Ld['lam'])
                    b.tt(lam[:, 0:64], lam[:, 0:64], lam[:, 64:128], ALU.mult)
                    b.tt(lam[:, 128:192], lam[:, 128:192], lam[:, 192:256], ALU.mult)
                    b.rsum(lsc[:, 0:1], lam[:, 0:64])
                    b.rsum(lsc[:, 1:2], lam[:, 128:192])
                    b.act(lsc[:, 2:4], lsc[:, 0:2], AF.Exp)
                    b.ts(lsc[:, 4:5], lsc[:, 3:4], -0.2, ALU.add)
                    b.tt(lsc[:, 5:6], lsc[:, 4:5], lsc[:, 2:3], ALU.subtract)
                    neglam = lsc[:, 5:6]
                    b.ts(lsc[:, 6:7], cv[:, 0:1], 0.8, ALU.mult)
                    sub8 = lsc[:, 6:7]
                    for h in range(4):
                        for wh in range(2):
                            for bi in range(NBLK):
                                bk = b.bank()
                                q = sq_next()
                                if wh == 0:
                                    b.act(q, qrT[:, h, blk(bi)], AF.Square)
                                    b.mm(ps[:, bk, :], ones[:], q)
                                else:
                                    q2 = sq_next()
                                    b.act(q, krT[:, 0, h, blk(bi)], AF.Square)
                                    b.act(q2, krT[:, 1, h, blk(bi)], AF.Square)
                                    b.mm(ps[:, bk, :], ones[:], q, start=True, stop=False)
                                    b.mm(ps[:, bk, :], ones[:], q2, start=False, stop=True)
                                b.rmax(mx[:, wh * 4 + bi:wh * 4 + bi + 1], ps[:, bk, :])
                            b.rmax(mx[:, 8 + wh:9 + wh], mx[:, wh * 4:wh * 4 + 4])
                        b.tt(mx[:, 10:11], mx[:, 8:9], mx[:, 9:10], ALU.mult)
                        b.act(mx[:, 11:12], mx[:, 10:11], AF.Sqrt)
                        b.ts(negM[:, h:h + 1], mx[:, 11:12], -0.125 * 1.02, ALU.mult)
                    evO = b.sb("evO", [128, 1, 2, 512], F32, s3)
                    d2b = b.sb("d2b", [128, 2, 512], F32, s3)
                    sqd = b.sb("sqd", [128, 2, 512], BF16, s3)
                    r0b = b.sb("r0b", [128, 2, 512], F32, s3)
                    r1b = b.sb("r1b", [128, 1, 512], F32, s3)
                    evS = b.sb("evS", [128, 1, 2, 512], F32, s3)
                    iters = []
                    for h in range(4):
                        for qb in range(NBLK):
                            def mk(h=h, qb=qb, idx=len(iters)):
                                nkt = 4 * qb + 4
                                bO = [0, 1]
                                bSm = [2, 3]

                                trim = qb > 0
                                korder = [4 * qb + d for d in range(4)] + list(range(4 * qb))

                                def cols(kt):
                                    d_ = kt - 4 * qb
                                    return 128 * d_ if (trim and d_ > 0) else 0

                                def qk(i, bank):
                                    kt, c = korder[i // 2], i % 2
                                    c0 = cols(kt)
                                    b.mm(ps[:, bank, c0:512], krT[:, c, h, kt * 128:(kt + 1) * 128], qrT[:, h, qb * 512 + c0:(qb + 1) * 512])

                                def ex(i, bank, pt):
                                    kt = korder[i // 2]
                                    d_ = kt - 4 * qb
                                    c0 = cols(kt)
                                    if d_ <= 0 or trim:
                                        b.act(pt[:, c0:], ps[:, bank, c0:], AF.Exp, bias=negM[:, h:h + 1], scale=0.125)
                                    else:
                                        b.memset(pt[:, 0:128 * d_], 0.0)
                                        b.act(pt[:, 128 * d_:], ps[:, bank, 128 * d_:], AF.Exp, bias=negM[:, h:h + 1], scale=0.125)
                                    if d_ >= 0:
                                        b.memset(pt[64:128, 128 * d_:128 * d_ + 64], 0.0)

                                def pv(i, pt):
                                    j_ = i // 2
                                    kt, c = korder[j_], i % 2
                                    c0 = cols(kt)
                                    b.mm(ps[:, bO[c], c0:512], vtm[:, kt, h * 128:(h + 1) * 128], pt[:, c0:], start=(j_ == 0), stop=(j_ == nkt - 1))
                                    b.mm(ps[:, bSm[c], c0:512], ones[:], pt[:, c0:], start=(j_ == 0), stop=(j_ == nkt - 1))

                                def post():
                                    e = 0
                                    for c in range(2):
                                        b.copy(evO[:, e, c, :], ps[:, bO[c], :], eng='dve')
                                        b.copy(evS[:, e, c, :], ps[:, bSm[c], :], eng='dve')
                                    for c in range(2):
                                        b.recip(evS[:, e, c, :], evS[:, e, c, :])
                                        b.tt(evO[:, e, c, :], evO[:, e, c, :], evS[:, e, c, :], ALU.mult)

                                def stA():
                                    b.stt(d2b[:, idx % 2, :], evO[:, 0, 1, :], neglam, evO[:, 0, 0, :], ALU.mult, ALU.add)
                                    b.tt(sqd[:, idx % 2, :], d2b[:, idx % 2, :], d2b[:, idx % 2, :], ALU.mult, eng='pool')

                                def stB():
                                    b.mm(ps[:, 7, :], ones[:], sqd[:, idx % 2, :])

                                def stC():
                                    b.rstd(r0b[:, idx % 2, :], ps[:, 7, :], 1.0 / 128, RMS_EPS, r1b[:, 0, :])

                                def stD():
                                    b.stt(yaT[:, h, blk(qb)], d2b[:, idx % 2, :], sub8, r0b[:, idx % 2, :], ALU.mult, ALU.mult)
                                post_slow = [(6, stA), (10, stB), (13, stC), (16, stD)]
                                return dict(n=2 * nkt, qk=qk, ex=ex, pv=pv, post=post, post_slow=post_slow)
                            iters.append(mk())
                    attn_pipeline(iters, [4, 5, 6], PT)
                qs.close()
                P.barrier()
                if dbg_stage == 'ya':
                    dump4(yaT)
                    return
                ybT = b.sb("ybT", [128, 4, S], BF16, ms)
                with ExitStack() as su:
                    upad = b.sb("upad", [128, 4, 30 + S], BF16, su)
                    b.memset(upad[:, :, 0:30], 0.0)
                    with ExitStack() as s1:
                        tTh = b.sb("tTh", [128, 8, 1024], BF16, s1)
                        wbuf = b.sb("wbuf", [128, 3, 8, 256], BF16, s1)
                        sgt = b.sb("sgt", [128, 2, 512], F32, s1)
                        wi = 0
                        for half in range(2):
                            for bl in range(2):
                                prenorm_block(0, 0, half * 2 + bl, tTh[:, :, bl * 512:(bl + 1) * 512])
                            for i in range(4):
                                wb = wbuf[:, wi % 3]
                                wi += 1
                                b.dma(wb, wview(Ld['w_in'], (8 + i) * 256, 256), q='pool')
                                for bl in range(2):
                                    bi = half * 2 + bl
                                    bA = b.bank()
                                    bB = b.bank()
                                    for kt in range(8):
                                        b.mm(ps[:, bA, :], wb[:, kt, 0:128], tTh[:, kt, bl * 512:(bl + 1) * 512], start=(kt == 0), stop=(kt == 7))
                                    for kt in range(8):
                                        b.mm(ps[:, bB, :], wb[:, kt, 128:256], tTh[:, kt, bl * 512:(bl + 1) * 512], start=(kt == 0), stop=(kt == 7))
                                    s_ = sgt[:, bl, :]
                                    b.act(s_, ps[:, bB, :], AF.Sigmoid)
                                    b.tt(upad[:, i, 30 + bi * 512:30 + (bi + 1) * 512], ps[:, bA, :], s_, ALU.mult)
                    P.barrier()
                    with ExitStack() as s2:
                        diag = b.sb("diag", [128, 124, 128], BF16, s2)
                        dwT = b.sb("dwT", [128, 124], F32, s2)
                        ysb = b.sb("ysb", [128, 4, 512], F32, s2)
                        ybf = b.sb("ybf", [128, 4, 512], BF16, s2)
                        st = b.sb("cst", [128, 4, 512], F32, s2)
                        b.dma(dwT[:], Ld['dwT'])
                        for j in range(124):
                            b.ts(diag[:, j, :], ident[:], dwT[:, j:j + 1], ALU.mult, eng='dve')
                        for bi in range(NBLK):
                            bM = b.bank()
                            bS = b.bank()
                            for ct in range(4):
                                bk = b.bank()
                                while bk in (bM, bS):
                                    bk = b.bank()
                                for j in range(31):
                                    b.mm(ps[:, bk, :], diag[:, ct * 31 + j, :], upad[:, ct, bi * 512 + j:bi * 512 + j + 512], start=(j == 0), stop=(j == 30))
                                b.act(ysb[:, ct, :], ps[:, bk, :], AF.Identity, bias=cv[:, 1 + ct:2 + ct])
                                b.copy(ybf[:, ct, :], ysb[:, ct, :], eng='dve')
                                q = sq_next()
                                b.act(q, ysb[:, ct, :], AF.Square)
                                b.mm(ps[:, bM, :], ones[:], ybf[:, ct, :], start=(ct == 0), stop=(ct == 3))
                                b.mm(ps[:, bS, :], ones[:], q, start=(ct == 0), stop=(ct == 3))
                            mean = st[:, 0, :]
                            m2 = st[:, 1, :]
                            var = st[:, 2, :]
                            rs = st[:, 3, :]
                            b.act(mean, ps[:, bM, :], AF.Copy, scale=1.0 / 512)
                            b.tt(m2, mean, mean, ALU.mult)
                            b.stt(var, ps[:, bS, :], 1.0 / 512, m2, ALU.mult, ALU.subtract)
                            b.rstd(rs, var, 1.0, LN_EPS, m2)
                            for ct in range(4):
                                t1 = tf_next()
                                b.tt(t1, ysb[:, ct, :], mean, ALU.subtract)
                                b.tt(t1, t1, rs, ALU.mult)
                                b.act(ybT[:, ct, blk(bi)], t1, AF.Silu, bias=cv[:, 9 + ct:10 + ct], scale=cv[:, 5 + ct:6 + ct])
                P.barrier()
                if dbg_stage == 'yb':
                    dump4(ybT)
                    return
                with ExitStack() as s4:
                    wo = b.sb("wo0", [128, 8, D], BF16, s4)
                    osb_[0] = b.sb("osb0", [128, 8, 512], F32, s4)
                    b.dma(wo[:, 0:4], wview(Ld['w_out'], 0, D)[:, 0:4], q='pool')
                    b.dma(wo[:, 4:8], wview(Ld['w_out'], 0, D)[:, 4:8], q='pool')
                    for bi in range(NBLK):
                        proj_block(wo, 8, lambda kt: (yaT if kt < 4 else ybT)[:, kt % 4, blk(bi)], 0, 8, bi)
            P.barrier()


        def dumpN(src, n):
            with ExitStack() as dsk:
                dbgT = b.sb("dbgT1", [128, n, S], F32, dsk)
                for ct in range(n):
                    b.copy(dbgT[:, ct, :], src[:, ct, :], eng='dve')
                b.dma(outT[0:128 * n, :].rearrange("(kt p) s -> p kt s", p=128), dbgT[:])
                P.barrier()

        def layer1_mixer():
            Ld = L[1]
            TWO_PI = 2.0 * math.pi
            with ExitStack() as ms:
                cv = b.sb("cv1", [128, 16], F32, ms)
                b.dma(cv[:], Ld['cv'])
                ycT = b.sb("ycT", [128, 2, S], BF16, ms)
                cqn = b.sb("cqn", [128, 2, S], BF16, ms)
                ckvn = b.sb("ckvn", [128, S], BF16, ms)
                krope = b.sb("krope", [128, 2, S], BF16, ms)
                b.memset(krope[64:128, 0], 0.0)
                b.memset(krope[0:64, 1], 0.0)
                with ExitStack() as ss:
                    uT = b.sb("uT", [128, 2, S], BF16, ss)
                    with ExitStack() as s1:
                        tTh = b.sb("tTh3", [128, 8, 1024], BF16, s1)
                        wb = b.sb("wbuf3", [128, 8, 896], BF16, s1)
                        rope = b.sb("rope1a", [128, 2, 1024], F32, s1)
                        cqf = b.sb("cqf", [128, 3, 512], F32, s1)
                        b.dma(wb[:, :, 0:512], wview(Ld['w_in'], 0, 512), q='pool')
                        b.dma(wb[:, :, 512:896], wview(Ld['w_in'], 512, 384), q='pool')
                        for half in range(2):
                            b.dma(rope[:], Ld['rope'][:, :, half * 1024:(half + 1) * 1024])
                            for bl in range(2):
                                prenorm_block(1, 0, half * 2 + bl, tTh[:, :, bl * 512:(bl + 1) * 512])
                            for bl in range(2):
                                bi = half * 2 + bl
                                tk = slice(bl * 512, (bl + 1) * 512)
                                banks = []
                                for m in range(7):
                                    bk = b.bank()
                                    banks.append(bk)
                                    for kt in range(8):
                                        b.mm(ps[:, bk, :], wb[:, kt, m * 128:(m + 1) * 128], tTh[:, kt, tk], start=(kt == 0), stop=(kt == 7))
                                    if m < 2:
                                        b.copy(uT[:, m, blk(bi)], ps[:, bk, :], eng='act')
                                    elif m < 5:
                                        b.copy(cqf[:, m - 2, :], ps[:, bk, :], eng=('act' if m % 2 else 'dve'))
                                bS = b.bank()
                                sumsq_mm(bS, [cqf[:, 0, :], cqf[:, 1, :]])
                                r = tf_next()
                                b.rstd(r, ps[:, bS, :], 1.0 / 256, RMS_EPS, tf_next())
                                for m in range(2):
                                    b.stt(cqn[:, m, blk(bi)], cqf[:, m, :], cv[:, m:m + 1], r, ALU.mult, ALU.mult)
                                bS = b.bank()
                                sumsq_mm(bS, [cqf[:, 2, :]])
                                r = tf_next()
                                b.rstd(r, ps[:, bS, :], 1.0 / 128, RMS_EPS, tf_next())
                                b.stt(ckvn[:, blk(bi)], cqf[:, 2, :], cv[:, 2:3], r, ALU.mult, ALU.mult)
                                t1 = tf_next()
                                t2 = tf_next()
                                b.tt(t1, ps[:, banks[5], :], rope[:, 0, tk], ALU.mult)
                                b.tt(t2, ps[:, banks[6], :], rope[:, 1, tk], ALU.mult)
                                b.tt(krope[0:64, 0, blk(bi)], t1[0:64, :], t2[0:64, :], ALU.add, eng='dve')
                                b.tt(krope[64:128, 1, blk(bi)], t1[64:128, :], t2[64:128, :], ALU.add, eng='dve')
                    P.barrier()
                    Cq = b.sb("Cq", [128, 16, 128], BF16, ss)
                    b.dma(Cq[:], Ld['Cpad'].rearrange("a r c -> r a c"), q='pool')
                    PW, PWW, r4 = S5P['PW'], S5P['PWW'], S5P['r4']
                    KdT = b.sb("KdT", [128, 4, 2, 128], BF16, ss)
                    ysb = b.sb("ysb1", [128, 2, S], F32, ss)
                    with ExitStack() as st_:
                        BTf = b.sb("BTf", [128, 2, 2, 128], F32, st_)
                        Ldb = b.sb("Ldb", [128, 2, 2, 4, 128], BF16, st_)
                        u4 = b.sb("u4", [128, 2, 4, 128], F32, st_)
                        for ct in range(2):
                            bks = [b.bank() for _ in range(4)]
                            for jj in range(4):
                                j = ct * 4 + jj
                                bt = BTf[:, jj % 2]
                                b.dma(bt[:, 0, :], Ld['BpadT'][j], q='sp')
                                b.dma(bt[:, 1, :], Ld['BpadT'][8 + j], q='sp')
                                btr = bt[:, 0, :].unsqueeze(1).to_broadcast([128, 4, 128])
                                bti = bt[:, 1, :].unsqueeze(1).to_broadcast([128, 4, 128])
                                hre = PW[:, 8:16:2, j:j + 1].to_broadcast([128, 4, 128])
                                him = PW[:, 9:16:2, j:j + 1].to_broadcast([128, 4, 128])
                                lb = Ldb[:, jj % 2]
                                b.tt(u4[:, 0], btr, hre, ALU.mult)
                                b.tt(u4[:, 1], bti, him, ALU.mult)
                                b.tt(lb[:, 0], u4[:, 0], u4[:, 1], ALU.subtract)
                                b.tt(u4[:, 0], btr, him, ALU.mult)
                                b.tt(u4[:, 1], bti, hre, ALU.mult)
                                b.stt(lb[:, 1], u4[:, 0], -1.0, u4[:, 1], ALU.mult, ALU.subtract)
                                for d_ in range(4):
                                    reg = ps[:, bks[d_], 0:128]
                                    b.mm(reg, lb[:, 0, d_, :], Cq[:, j, :], start=(jj == 0), stop=False)
                                    b.mm(reg, lb[:, 1, d_, :], Cq[:, 8 + j, :], start=False, stop=(jj == 3))
                            b.stt(KdT[:, 0, ct, :], ident[:], cv[:, 3 + ct:4 + ct], ps[:, bks[0], 0:128], ALU.mult, ALU.add)
                            for d_ in range(1, 4):
                                b.copy(KdT[:, d_, ct, :], ps[:, bks[d_], 0:128], eng='act')
                    P.barrier()
                    for ct in range(2):
                        with ExitStack() as sct:
                            c4s = slice(ct * 4, ct * 4 + 4)
                            Rre = b.sb(f"Rre{ct}", [128, 4, 512], F32, sct)
                            Rim = b.sb(f"Rim{ct}", [128, 4, 512], F32, sct)
                            BH = b.sb(f"BH{ct}", [128, 4, 4, 2, 128], BF16, sct)
                            Cs = b.sb(f"Cs{ct}", [128, 4, 4, 2, 128], BF16, sct)
                            cz = b.sb(f"cz{ct}", [128, 1, 2, 512], F32, sct)
                            xx = b.sb(f"xx{ct}", [128, 2, 2, 516], BF16, sct)
                            b.memset(xx[:, :, :, 0:1], 0.0)
                            b.memset(Rre[:, :, 0:1], 1.0, eng='dve')
                            b.memset(Rim[:, :, 0:1], 0.0, eng='dve')
                            with ExitStack() as st_:
                                tA = b.sb(f"tabA{ct}", [128, 4, 256], F32, st_)
                                tB = b.sb(f"tabB{ct}", [128, 4, 256], F32, st_)
                                m = 1
                                lv = 0
                                while m < 512:
                                    wrb = PWW[:, lv, 0, c4s].unsqueeze(2).to_broadcast([128, 4, m])
                                    wib = PWW[:, lv, 1, c4s].unsqueeze(2).to_broadcast([128, 4, m])
                                    a_, c_ = tA[:, :, 0:m], tB[:, :, 0:m]
                                    b.tt(a_, Rre[:, :, 0:m], wrb, ALU.mult)
                                    b.tt(c_, Rim[:, :, 0:m], wib, ALU.mult)
                                    b.tt(Rre[:, :, m:2 * m], a_, c_, ALU.subtract)
                                    b.tt(a_, Rre[:, :, 0:m], wib, ALU.mult)
                                    b.tt(c_, Rim[:, :, 0:m], wrb, ALU.mult)
                                    b.tt(Rim[:, :, m:2 * m], a_, c_, ALU.add)
                                    m *= 2
                                    lv += 1
                                Bf = b.sb(f"Bf{ct}", [128, 2, 2, 128], F32, st_)
                                Cf = b.sb(f"Cf{ct}", [128, 2, 2, 128], F32, st_)
                                dg = b.sb(f"dgf{ct}", [128, 8, 128], F32, st_)
                                dgh = b.sb(f"dgh{ct}", [128, 8, 128], BF16, st_)
                                dgl = b.sb(f"dgl{ct}", [128, 8, 128], BF16, st_)
                                hrow = b.sb(f"hrow{ct}", [128, 8, 128], F32, st_)
                                t4a = tA[:, 0:2, :].rearrange("p a (c n) -> p (a c) n", n=128)
                                t4b = tB[:, 0:2, :].rearrange("p a (c n) -> p (a c) n", n=128)
                                hr4 = hrow[:].rearrange("p (d c) n -> p d c n", c=2)
                                idb = ident[:].unsqueeze(1).to_broadcast([128, 8, 128])
                                for jj in range(4):
                                    j = ct * 4 + jj
                                    Bfj, Cfj = Bf[:, jj % 2], Cf[:, jj % 2]
                                    b.dma(Bfj[:, 0, :], Ld['Bpad'][j], q='sp')
                                    b.dma(Bfj[:, 1, :], Ld['Bpad'][8 + j], q='sp')
                                    b.dma(Cfj[:, 0, :], Ld['Cpad'][j], q='sp')
                                    b.dma(Cfj[:, 1, :], Ld['Cpad'][8 + j], q='sp')
                                    b.tt(dg[:], idb, PW[:, 8:16, j:j + 1].to_broadcast([128, 8, 128]), ALU.mult)
                                    b.copy(dgh[:], dg[:], eng='dve')
                                    b.tt(dgl[:], dg[:], dgh[:], ALU.subtract)
                                    for half in range(2):
                                        bk = b.bank()
                                        while bk >= 4:
                                            bk = b.bank()
                                        b.mm(ps[:, bk, :], ones[:], dgh[:, 4 * half:4 * half + 4, :], start=True, stop=False)
                                        b.mm(ps[:, bk, :], ones[:], dgl[:, 4 * half:4 * half + 4, :], start=False, stop=True)
                                        b.copy(hrow[:, 4 * half:4 * half + 4, :], ps[:, bk, :].rearrange("p (a n) -> p a n", n=128), eng='act')
                                    bre_b = Bfj[:, 0, :].unsqueeze(1).to_broadcast([128, 4, 128])
                                    bim_b = Bfj[:, 1, :].unsqueeze(1).to_broadcast([128, 4, 128])
                                    b.tt(t4a, bre_b, hr4[:, :, 0, :], ALU.mult)
                                    b.tt(t4b, bim_b, hr4[:, :, 1, :], ALU.mult)
                                    b.tt(BH[:, jj, :, 0, :], t4a, t4b, ALU.subtract)
                                    b.tt(t4a, bre_b, hr4[:, :, 1, :], ALU.mult)
                                    b.tt(t4b, bim_b, hr4[:, :, 0, :], ALU.mult)
                                    b.tt(BH[:, jj, :, 1, :], t4a, t4b, ALU.add)
                                    cre_b = Cfj[:, 0, :].unsqueeze(1).to_broadcast([128, 4, 128])
                                    cim_b = Cfj[:, 1, :].unsqueeze(1).to_broadcast([128, 4, 128])
                                    prb = PW[:, 0:8:2, j:j + 1].to_broadcast([128, 4, 128])
                                    pib = PW[:, 1:8:2, j:j + 1].to_broadcast([128, 4, 128])
                                    t4c = tA[:, 2:4, :].rearrange("p a (c n) -> p (a c) n", n=128)
                                    t4d = tB[:, 2:4, :].rearrange("p a (c n) -> p (a c) n", n=128)
                                    b.tt(t4c, cre_b, prb, ALU.mult, eng='pool')
                                    b.tt(t4d, cim_b, pib, ALU.mult, eng='pool')
                                    b.tt(Cs[:, jj, :, 0, :], t4c, t4d, ALU.subtract, eng='pool')
                                    b.tt(t4c, cre_b, pib, ALU.mult, eng='pool')
                                    b.tt(t4d, cim_b, prb, ALU.mult, eng='pool')
                                    b.tt(Cs[:, jj, :, 1, :], t4c, t4d, ALU.add, eng='pool')
                            for s_ in range(4):
                                for sp_ in range(s_ + 1):
                                    b.mm(ps[:, 4 + s_, :], KdT[:, s_ - sp_, ct, :], uT[:, ct, sp_:S:4], start=(sp_ == 0), stop=False)
                            for jj in range(4):
                                j = ct * 4 + jj
                                k_ = jj % 2
                                bre, bim = k_ * 2, k_ * 2 + 1
                                for s_ in range(4):
                                    b.mm(ps[:, bre, :], BH[:, jj, 3 - s_, 0, :], uT[:, ct, s_:S:4], start=(s_ == 0), stop=(s_ == 3))
                                for s_ in range(4):
                                    b.mm(ps[:, bim, :], BH[:, jj, 3 - s_, 1, :], uT[:, ct, s_:S:4], start=(s_ == 0), stop=(s_ == 3))
                                t1, t2, t3, t4 = tf_next(), tf_next(), tf_next(), tf_next()
                                b.tt(t1, ps[:, bre, :], Rre[:, jj, :], ALU.mult)
                                b.tt(t2, ps[:, bim, :], Rim[:, jj, :], ALU.mult)
                                b.tt(t3, ps[:, bim, :], Rre[:, jj, :], ALU.mult)
                                b.tt(t4, ps[:, bre, :], Rim[:, jj, :], ALU.mult)
                                b.tt(cz[:, 0, 0, :], t1, t2, ALU.subtract, eng='pool')
                                b.tt(cz[:, 0, 1, :], t3, t4, ALU.add, eng='pool')
                                rb = r4[:, j:j + 1].to_broadcast([128, 512])
                                b.scan(cz[:, 0, 0, :], rb, cz[:, 0, 0, :], 0.0)
                                b.scan(cz[:, 0, 1, :], rb, cz[:, 0, 1, :], 0.0)
                                t1, t2, t3, t4 = tf_next(), tf_next(), tf_next(), tf_next()
                                b.tt(t1, cz[:, 0, 0, :], Rre[:, jj, :], ALU.mult)
                                b.tt(t2, cz[:, 0, 1, :], Rim[:, jj, :], ALU.mult)
                                b.tt(t3, cz[:, 0, 0, :], Rim[:, jj, :], ALU.mult)
                                b.tt(t4, cz[:, 0, 1, :], Rre[:, jj, :], ALU.mult)
                                b.tt(xx[:, k_, 0, 1:513], t1, t2, ALU.add, eng='pool')
                                b.tt(xx[:, k_, 1, 1:513], t3, t4, ALU.subtract, eng='pool')
                                for s_ in range(4):
                                    b.mm(ps[:, 4 + s_, :], Cs[:, jj, s_, 0, :], xx[:, k_, 0, 0:512], start=False, stop=False)
                                    b.mm(ps[:, 4 + s_, :], Cs[:, jj, s_, 1, :], xx[:, k_, 1, 0:512], start=False, stop=(jj == 3))
                            for s_ in range(4):
                                b.copy(ysb[:, ct, s_:S:4], ps[:, 4 + s_, :], eng=('act' if s_ % 2 else 'dve'))
                        P.barrier()
                    wg = b.sb("wglu", [128, 2, 256], BF16, ss)
                    b.dma(wg[:], wview(Ld['w_glu'], 0, 256), q='pool')
                    zbf = b.sb("zbf", [128, 2, S], BF16, ss)
                    for bi in range(NBLK):
                        for ct in range(2):
                            y_ = ysb[:, ct, blk(bi)]
                            t1, t2 = tf_next(), tf_next()
                            b.tt(t1, y_, y_, ALU.mult)
                            b.ts(t1, t1, 0.044715, ALU.mult, 1.0, ALU.add)
                            b.tt(t1, t1, y_, ALU.mult)
                            b.act(t2, t1, AF.Sigmoid, scale=1.5957691216057308)
                            b.tt(y_, y_, t2, ALU.mult)
                            b.copy(zbf[:, ct, blk(bi)], y_, eng='dve')
                        for mt in range(2):
                            bk = b.bank()
                            for kt in range(2):
                                b.mm(ps[:, bk, :], wg[:, kt, mt * 128:(mt + 1) * 128], zbf[:, kt, blk(bi)], start=(kt == 0), stop=(kt == 1))
                            t2 = tf_next()
                            b.act(t2, ps[:, bk, :], AF.Sigmoid, bias=cv[:, 5 + mt:6 + mt])
                            b.tt(ycT[:, mt, blk(bi)], ysb[:, mt, blk(bi)], t2, ALU.mult)
                P.barrier()
                if dbg_stage == 'yc':
                    dumpN(ycT, 2)
                    return
                ydT = b.sb("ydT", [128, 6, S], BF16, ms)
                qrT = b.sb("qrT1", [128, 3, S], BF16, ms)
                wuq = b.sb("wuq", [128, 2, 1536], BF16, ms)
                wukv = b.sb("wukv", [128, 1536], BF16, ms)
                b.dma(wuq[:], wview(Ld['w_uq'], 0, 1536), q='pool')
                b.dma(wukv[:], Ld['w_ukv'], q='pool')
                with ExitStack() as s1:
                    rope = b.sb("rope1b", [128, 2, S], F32, s1)
                    b.dma(rope[:], Ld['rope'])
                    for pr in range(3):
                        for bi in range(NBLK):
                            bA, bB = b.bank(), b.bank()
                            for kt in range(2):
                                b.mm(ps[:, bA, :], wuq[:, kt, 768 + pr * 128:768 + (pr + 1) * 128], cqn[:, kt, blk(bi)], start=(kt == 0), stop=(kt == 1))
                            for kt in range(2):
                                b.mm(ps[:, bB, :], wuq[:, kt, 1152 + pr * 128:1152 + (pr + 1) * 128], cqn[:, kt, blk(bi)], start=(kt == 0), stop=(kt == 1))
                            t1, t2 = tf_next(), tf_next()
                            b.tt(t1, ps[:, bA, :], rope[:, 0, blk(bi)], ALU.mult)
                            b.tt(t2, ps[:, bB, :], rope[:, 1, blk(bi)], ALU.mult)
                            b.tt(qrT[:, pr, blk(bi)], t1, t2, ALU.add, eng='dve')
                P.barrier()
                with ExitStack() as s3:
                    PT = b.sb("PT1", [128, 4, 512], BF16, s3)
                    mx = b.sb("mx1", [128, 2, 16], F32, s3)
                    negM = b.sb("negM1", [128, 6], F32, s3)
                    rr = b.sb("rr1", [128, 2, 512], F32, s3)
                    qnT = b.sb("qnT", [128, 2, S], BF16, s3)
                    knT = b.sb("knT", [128, 2, S], BF16, s3)
                    vt = b.sb("vt1", [128, 2, 16, 128], BF16, s3)
                    SC = 192.0 ** -0.5
                    evO = b.sb("evO1", [128, 2, 512], F32, s3)
                    evS = b.sb("evS1", [128, 2, 512], F32, s3)
                    iters = []
                    units_of = {}
                    sqm = b.sb("sqm", [128, 2, 2, 512], BF16, s3)
                    for h in range(6):
                        hb = h % 2
                        hp = slice(64 * (h % 2), 64 * (h % 2) + 64)

                        def make_units(h=h, hb=hb, hp=hp):
                            U = []
                            pb = [7, 7]
                            cnt = [0]

                            def nb():
                                cnt[0] += 1
                                return pb[cnt[0] % 2]
                            for bi in range(NBLK):
                                def uq(bi=bi):
                                    bk = nb()
                                    b.mm(ps[:, bk, :], wuq[:, 0, h * 128:(h + 1) * 128], cqn[:, 0, blk(bi)], start=True, stop=False)
                                    b.mm(ps[:, bk, :], wuq[:, 1, h * 128:(h + 1) * 128], cqn[:, 1, blk(bi)], start=False, stop=True)
                                    b.copy(qnT[:, hb, blk(bi)], ps[:, bk, :], eng='dve')
                                U.append(uq)

                                def uk(bi=bi):
                                    bk = nb()
                                    b.mm(ps[:, bk, :], wukv[:, h * 128:(h + 1) * 128], ckvn[:, blk(bi)])
                                    b.copy(knT[:, hb, blk(bi)], ps[:, bk, :], eng='dve')
                                U.append(uk)
                            for t4_ in range(4):
                                def uv(t4_=t4_):
                                    bk = nb()
                                    for u_ in range(4):
                                        tti = t4_ * 4 + u_
                                        b.mm(ps[:, bk, u_ * 128:(u_ + 1) * 128], ckvn[:, tti * 128:(tti + 1) * 128], wukv[:, 768 + h * 128:768 + (h + 1) * 128])
                                    b.copy(vt[:, hb, t4_ * 4:(t4_ + 1) * 4, :], ps[:, bk, :].rearrange("p (a c) -> p a c", c=128), eng='dve')
                                U.append(uv)
                            pend = []
                            for wh in range(2):
                                for bi in range(NBLK):
                                    def usq(wh=wh, bi=bi):
                                        k_ = (wh * 4 + bi) % 2
                                        q1, q2 = sqm[:, k_, 0, :], sqm[:, k_, 1, :]
                                        if wh == 0:
                                            b.tt(q1, qnT[:, hb, blk(bi)], qnT[:, hb, blk(bi)], ALU.mult, eng='pool')
                                            b.tt(q2[hp, :], qrT[hp, h // 2, blk(bi)], qrT[hp, h // 2, blk(bi)], ALU.mult, eng='pool')
                                        else:
                                            b.tt(q1, knT[:, hb, blk(bi)], knT[:, hb, blk(bi)], ALU.mult, eng='pool')
                                            b.tt(q2[hp, :], krope[hp, h % 2, blk(bi)], krope[hp, h % 2, blk(bi)], ALU.mult, eng='pool')

                                    def umm(wh=wh, bi=bi):
                                        k_ = (wh * 4 + bi) % 2
                                        q1, q2 = sqm[:, k_, 0, :], sqm[:, k_, 1, :]
                                        bk = nb()
                                        b.mm(ps[:, bk, :], ones[:], q1, start=True, stop=False)
                                        b.mm(ps[:, bk, :], ones[hp, :], q2[hp, :], start=False, stop=True)
                                        b.rmax(mx[:, hb, wh * 4 + bi:wh * 4 + bi + 1], ps[:, bk, :])
                                    pend.append((usq, umm))
                            prev = None
                            for (usq, umm) in pend:
                                def unit(usq=usq, prev=prev):
                                    usq()
                                    if prev is not None:
                                        prev()
                                U.append(unit)
                                prev = umm

                            def ufin(prev=prev):
                                prev()
                                b.rmax(mx[:, hb, 8:9], mx[:, hb, 0:4])
                                b.rmax(mx[:, hb, 9:10], mx[:, hb, 4:8])
                                b.tt(mx[:, hb, 10:11], mx[:, hb, 8:9], mx[:, hb, 9:10], ALU.mult)
                                b.act(mx[:, hb, 11:12], mx[:, hb, 10:11], AF.Sqrt)
                                b.ts(negM[:, h:h + 1], mx[:, hb, 11:12], -SC * 1.02, ALU.mult)
                            U.append(ufin)
                            return U
                        units_of[h] = make_units()

                        def prep(h=h):
                            for u in units_of[h]:
                                u()
                            units_of[h] = []

                        for qb in range(NBLK):
                            def mk(h=h, hb=hb, hp=hp, qb=qb, idx=len(iters), prep=prep):
                                nkt = 4 * qb + 4
                                bO = 2 + 2 * (idx % 2)
                                bSm = 3 + 2 * (idx % 2)
                                if idx % 2 == 0:
                                    bO, bSm = 2, 3
                                else:
                                    bO, bSm = 0, 1

                                trim = qb > 0
                                korder = [4 * qb + d for d in range(4)] + list(range(4 * qb))

                                def cols(kt):
                                    d_ = kt - 4 * qb
                                    return 128 * d_ if (trim and d_ > 0) else 0

                                def qk(i, bank):
                                    kt = korder[i]
                                    c0 = cols(kt)
                                    ks = slice(kt * 128, (kt + 1) * 128)
                                    qs_ = slice(qb * 512 + c0, (qb + 1) * 512)
                                    b.mm(ps[:, bank, c0:512], knT[:, hb, ks], qnT[:, hb, qs_], start=True, stop=False)
                                    b.mm(ps[:, bank, c0:512], krope[:, h % 2, ks], qrT[:, h // 2, qs_], start=False, stop=True)

                                def ex(i, bank, pt):
                                    kt = korder[i]
                                    d_ = kt - 4 * qb
                                    c0 = cols(kt)
                                    if d_ <= 0 or trim:
                                        b.act(pt[:, c0:], ps[:, bank, c0:], AF.Exp, bias=negM[:, h:h + 1], scale=SC)
                                    else:
                                        b.memset(pt[:, 0:128 * d_], 0.0)
                                        b.act(pt[:, 128 * d_:], ps[:, bank, 128 * d_:], AF.Exp, bias=negM[:, h:h + 1], scale=SC)
                                    if d_ >= 0:
                                        b.memset(pt[64:128, 128 * d_:128 * d_ + 64], 0.0)

                                def pv(i, pt):
                                    kt = korder[i]
                                    c0 = cols(kt)
                                    b.mm(ps[:, bO, c0:512], vt[:, hb, kt, :], pt[:, c0:], start=(i == 0), stop=(i == nkt - 1))
                                    b.mm(ps[:, bSm, c0:512], ones[:], pt[:, c0:], start=(i == 0), stop=(i == nkt - 1))

                                def post():
                                    e = idx % 2
                                    b.copy(evO[:, e, :], ps[:, bO, :], eng='dve')
                                    b.copy(evS[:, e, :], ps[:, bSm, :], eng='dve')
                                    b.recip(evS[:, e, :], evS[:, e, :])
                                    b.tt(ydT[:, h, blk(qb)], evO[:, e, :], evS[:, e, :], ALU.mult)
                                return dict(n=nkt, qk=qk, ex=ex, pv=pv, post=post, pre=(prep if qb == 0 else None), bg=(lambda: units_of.get(h + 1)))
                            iters.append(mk())
                    attn_pipeline(iters, [4, 5, 6], PT, early_pre=False)
                P.barrier()
                if dbg_stage == 'yd':
                    dumpN(ydT, 6)
                    return
                with ExitStack() as s4:
                    wo = b.sb("wo1", [128, 8, D], BF16, s4)
                    osb_[0] = b.sb("osb1", [128, 8, 512], F32, s4)
                    b.dma(wo[:, 0:4], wview(Ld['w_out'], 0, D)[:, 0:4], q='pool')
                    b.dma(wo[:, 4:8], wview(Ld['w_out'], 0, D)[:, 4:8], q='pool')
                    for bi in range(NBLK):
                        proj_block(wo, 8, lambda kt: (ycT[:, kt, blk(bi)] if kt < 2 else ydT[:, kt - 2, blk(bi)]), 1, 8, bi)
            P.barrier()

        if dbg_stage in ('yc', 'yd', 'l1only', 'h3l1'):
            s5_params()
            layer1_mixer()
            if dbg_stage in ('l1only', 'h3l1'):
                b.dma(outT.rearrange("(kt p) s -> p kt s", p=128), hT[:])
        else:
            s5_params()
            layer0_mixer()
            if dbg_stage not in ('ya', 'yb'):
                if dbg_stage != 'h1':
                    ffn(0)
                if dbg_stage is None:
                    layer1_mixer()
                    ffn(1)
                b.dma(outT.rearrange("(kt p) s -> p kt s", p=128), hT[:])
        P.finalize(sems)
        nc._prog_stats = P.stats
    return nc


def _cols(v, n):
    return np.ascontiguousarray(np.asarray(v, np.float32).reshape(n, 128).T)


def prep_inputs(inp):
    f = np.float32
    shared = {}
    shared['ident'] = np.eye(128, dtype=f)
    for l in range(2):
        vec = np.concatenate([_cols(inp[f'l{l}_mix_pre'], 8), _cols(inp[f'l{l}_mix_post'], 8),
                              _cols(inp[f'l{l}_ffn_pre'], 8), _cols(inp[f'l{l}_ffn_post'], 8)], axis=1)
        shared[f'l{l}_vec'] = np.ascontiguousarray(vec, f)
        shared[f'l{l}_w_out'] = np.ascontiguousarray(inp[f'l{l}_w_out'], f)
        wg_ = np.asarray(inp[f'l{l}_w_gate'], f).reshape(8, 128, NFT, 128)
        wu_ = np.asarray(inp[f'l{l}_w_up'], f).reshape(8, 128, NFT, 128)
        gu = np.concatenate([wg_.transpose(2, 1, 0, 3), wu_.transpose(2, 1, 0, 3)], axis=3)
        shared[f'l{l}_wgu'] = np.ascontiguousarray(gu.reshape(NFT, 128, 8 * 256))
        wd_ = np.asarray(inp[f'l{l}_w_down'], f).reshape(NFT, 128, 8, 128)
        shared[f'l{l}_wd'] = np.ascontiguousarray(wd_.transpose(2, 1, 0, 3).reshape(8, 128, NFT * 128))
    w = np.asarray(inp['l0_w_in'], f)
    perm64 = np.arange(64)
    perm64[0:8] = np.arange(8, 16)
    perm64[8:16] = np.arange(0, 8)
    cols = []
    for base in (0, 512):
        for i in range(4):
            c0 = base + 128 * i
            cols.append(np.arange(c0, c0 + 128))
            cols.append(np.concatenate([c0 + perm64, c0 + 64 + perm64]))
    for i in range(4):
        cols.append(np.arange(1536 + 128 * i, 1536 + 128 * (i + 1)))
        cols.append(np.arange(2048 + 128 * i, 2048 + 128 * (i + 1)))
    cols.append(np.arange(1024, 1536))
    cols = np.concatenate(cols)
    shared['l0_w_inx'] = np.ascontiguousarray(w[:, cols])
    rot = 16
    inv = 500000.0 ** (-np.arange(0, rot, 2, dtype=np.float64) / rot)
    ang = np.arange(S, dtype=np.float64)[:, None] * inv[None, :]
    cs, sn = np.cos(ang).astype(f), np.sin(ang).astype(f)
    C = np.ones((128, S), f)
    Sg = np.zeros((128, S), f)
    for p in range(128):
        d = p % 64
        if d < 8:
            C[p] = cs[:, d]
            Sg[p] = -sn[:, d]
        elif d < 16:
            C[p] = cs[:, d - 8]
            Sg[p] = sn[:, d - 8]
    shared['l0_rope'] = np.ascontiguousarray(np.stack([C, Sg], axis=1))
    lamv = np.concatenate([inp['l0_lambda_q1'], inp['l0_lambda_k1'], inp['l0_lambda_q2'], inp['l0_lambda_k2']]).astype(f)
    shared['l0_lam'] = np.ascontiguousarray(np.tile(lamv[None, :], (128, 1)))
    cvv = np.zeros((128, 16), f)
    cvv[:, 0] = inp['l0_subln']
    cvv[:, 1:5] = _cols(inp['l0_dw_b'], 4)
    cvv[:, 5:9] = _cols(inp['l0_conv_ln_g'], 4)
    cvv[:, 9:13] = _cols(inp['l0_conv_ln_b'], 4)
    shared['l0_cv'] = cvv
    dw = np.asarray(inp['l0_dw_w'], f)
    dwT = dw.T.reshape(4, 128, 31).transpose(1, 0, 2).reshape(128, 124)
    shared['l0_dwT'] = np.ascontiguousarray(dwT)
    w = np.asarray(inp['l1_w_in'], f)
    p64 = np.concatenate([np.arange(32, 64), np.arange(0, 32)])
    kr = w[:, 640:704]
    shared['l1_w_inx'] = np.ascontiguousarray(np.concatenate([w[:, 0:640], kr, kr, kr[:, p64], kr[:, p64]], axis=1))
    wq = np.asarray(inp['l1_w_uq'], f).reshape(256, 6, 192)
    nope = wq[:, :, 0:128].reshape(256, 768)
    rp = wq[:, :, 128:192]
    shared['l1_w_uqx'] = np.ascontiguousarray(np.concatenate([nope, rp.reshape(256, 384), rp[:, :, p64].reshape(256, 384)], axis=1))
    wkv = np.asarray(inp['l1_w_ukv'], f).reshape(128, 6, 256)
    shared['l1_w_ukvx'] = np.ascontiguousarray(np.concatenate([wkv[:, :, 0:128].reshape(128, 768), wkv[:, :, 128:256].reshape(128, 768)], axis=1))
    shared['l1_w_glu'] = np.ascontiguousarray(inp['l1_w_glu'], f)
    inv = 10000.0 ** (-np.arange(0, 64, 2, dtype=np.float64) / 64)
    ang = np.arange(S, dtype=np.float64)[:, None] * inv[None, :]
    cs, sn = np.cos(ang).astype(f), np.sin(ang).astype(f)
    C = np.zeros((128, S), f)
    Sg = np.zeros((128, S), f)
    for p in range(128):
        d = p % 64
        C[p] = cs[:, d % 32]
        Sg[p] = -sn[:, d] if d < 32 else sn[:, d - 32]
    shared['l1_rope'] = np.ascontiguousarray(np.stack([C, Sg], axis=1))
    cv1 = np.zeros((128, 16), f)
    cv1[:, 0:2] = _cols(inp['l1_q_norm'], 2)
    cv1[:, 2] = inp['l1_kv_norm']
    cv1[:, 3:5] = _cols(np.asarray(inp['l1_d_skip'], f).reshape(-1), 2)
    cv1[:, 5:7] = _cols(inp['l1_b_glu'], 2)
    shared['l1_cv'] = cv1
    ssm = np.concatenate([_cols(np.asarray(inp['l1_a_re'], f).reshape(-1), 8),
                          _cols(np.asarray(inp['l1_a_im'], f).reshape(-1), 8),
                          _cols(np.repeat(np.asarray(inp['l1_log_dt'], f), 64), 8)], axis=1)
    shared['l1_ssm'] = np.ascontiguousarray(ssm, f)
    Bp = np.zeros((2, 8, 128, 128), f)
    Cp = np.zeros((2, 8, 128, 128), f)
    for a, (bsrc, csrc) in enumerate(((inp['l1_b_re'], inp['l1_c_re']), (inp['l1_b_im'], inp['l1_c_im']))):
        bsrc = np.asarray(bsrc, f)
        csrc = np.asarray(csrc, f)
        for g in range(16):
            j, e = g // 2, g % 2
            r0 = 32 * (j % 4) + 16 * e
            Bp[a, j, r0:r0 + 16, 64 * e:64 * e + 64] = bsrc[g].T
            Cp[a, j, 64 * e:64 * e + 64, r0:r0 + 16] = csrc[g].T
    shared['l1_Bpad'] = np.ascontiguousarray(Bp.reshape(16, 128, 128))
    shared['l1_Cpad'] = np.ascontiguousarray(Cp.reshape(16, 128, 128))
    shared['l1_BpadT'] = np.ascontiguousarray(Bp.transpose(0, 1, 3, 2).reshape(16, 128, 128))
    return shared


_NC_CACHE = {}


def kernel(**inputs):
    dbg = os.environ.get('KDBG') or None
    inp = {k: np.asarray(v) for k, v in inputs.items()}
    shared = prep_inputs(inp)
    x = np.asarray(inp['x'], np.float32)
    ncores = 8
    if dbg not in _NC_CACHE:
        _NC_CACHE[dbg] = build(dbg)
    nc = _NC_CACHE[dbg]
    in_maps = []
    for c in range(ncores):
        m = dict(shared)
        m['xT'] = np.ascontiguousarray(x[c].T)
        in_maps.append(m)
    res = run_bass_kernel_spmd(nc, in_maps, core_ids=list(range(ncores)))
    out = np.stack([np.ascontiguousarray(r['outT'].T) for r in res.results], axis=0)
    return out.astype(np.float32)
```
